# Optimizing a Trainium2 kernel written in Bass

```python
import math
import jax, jax.numpy as jnp
from jax import lax
import numpy as np

D_MODEL = 1024
BATCH = 32
SEQ = 2048
DEPTH = 2

D_MIX = D_MODEL
D_A = D_MIX // 2
D_B = D_MIX - D_A
N_HEADS_A = 8
HEAD_DIM_A = D_A // N_HEADS_A
N_GROUPS_B = 8
CHUNK = 128
CONV_WIDTH = 3
D_FF = 2816
N_MOD = 9
D_IN_PROJ = 2 * D_A + 3 * D_B
EPS = 1e-6

kernel_name = "hybrid_sgu_shortconv_macaron_adaln"


def rms_norm(x, g):
    xf = x.astype(jnp.float32)
    y = xf * lax.rsqrt(jnp.mean(xf * xf, axis=-1, keepdims=True) + EPS)
    return (y * g.astype(jnp.float32)).astype(x.dtype)


def layer_norm(x, g, b):
    xf = x.astype(jnp.float32)
    mu = jnp.mean(xf, axis=-1, keepdims=True)
    var = jnp.mean(jnp.square(xf - mu), axis=-1, keepdims=True)
    y = (xf - mu) * lax.rsqrt(var + EPS)
    return (y * g.astype(jnp.float32) + b.astype(jnp.float32)).astype(x.dtype)


def modulate(h, shift, scale):
    return h * (1 + scale[:, None, :]) + shift[:, None, :]


def swiglu_ffn(h, w_gu, w_down):
    gu = jnp.einsum('bsd,df->bsf', h, w_gu)
    g, u = jnp.split(gu, 2, axis=-1)
    return jnp.einsum('bsf,fd->bsd', jax.nn.silu(g) * u, w_down)


def chunked_sgu(u, v, ln_g, ln_b, w_s, b_s):
    bsz, s, _ = v.shape
    n_chunks = s // CHUNK
    v = layer_norm(v.reshape(bsz, s, N_HEADS_A, HEAD_DIM_A), ln_g, ln_b)
    v = v.reshape(bsz, n_chunks, CHUNK, N_HEADS_A, HEAD_DIM_A)
    causal = jnp.tril(jnp.ones((CHUNK, CHUNK), dtype=bool))
    w_masked = jnp.where(causal[None], w_s, jnp.zeros_like(w_s))
    mixed = jnp.einsum('hts,bcshd->bcthd', w_masked, v)
    mixed = mixed + jnp.transpose(b_s)[None, None, :, :, None]
    return u * mixed.reshape(bsz, s, D_A)


def short_gated_conv(b_gate, c_gate, xb, conv_w):
    s = xb.shape[1]
    z = c_gate * xb
    zp = jnp.pad(z, ((0, 0), (CONV_WIDTH - 1, 0), (0, 0)))
    conv = zp[:, 0:s] * conv_w[0] + zp[:, 1:s + 1] * conv_w[1] + zp[:, 2:s + 2] * conv_w[2]
    return b_gate * conv


def setup_inputs(seed: int = 0) -> dict:
    key = jax.random.key(seed)
    ks = jax.random.split(key, 24)

    def nrm(k, shape, scale):
        return scale * jax.random.normal(k, shape, jnp.float32)

    def gain(k, shape):
        return 1.0 + 0.02 * jax.random.normal(k, shape, jnp.float32)

    return {
        "x": nrm(ks[0], (BATCH, SEQ, D_MODEL), 1.0),
        "c": nrm(ks[1], (BATCH, D_MODEL), 1.0),
        "ada_w": nrm(ks[2], (DEPTH, D_MODEL, N_MOD * D_MODEL), 0.5 * D_MODEL ** -0.5),
        "ada_b": nrm(ks[3], (DEPTH, N_MOD * D_MODEL), 0.01),
        "norm_ffn1_g": gain(ks[4], (DEPTH, D_MODEL)),
        "ffn1_w_gu": nrm(ks[5], (DEPTH, D_MODEL, 2 * D_FF), D_MODEL ** -0.5),
        "ffn1_w_down": nrm(ks[6], (DEPTH, D_FF, D_MODEL), D_FF ** -0.5),
        "norm_mix_g": gain(ks[7], (DEPTH, D_MODEL)),
        "mix_w_in": nrm(ks[8], (DEPTH, D_MODEL, D_IN_PROJ), D_MODEL ** -0.5),
        "sgu_ln_g": gain(ks[9], (DEPTH, HEAD_DIM_A)),
        "sgu_ln_b": nrm(ks[10], (DEPTH, HEAD_DIM_A), 0.02),
        "sgu_w_s": nrm(ks[11], (DEPTH, N_HEADS_A, CHUNK, CHUNK), CHUNK ** -0.5),
        "sgu_b": gain(ks[12], (DEPTH, N_HEADS_A, CHUNK)),
        "conv_w": nrm(ks[13], (DEPTH, CONV_WIDTH, D_B), CONV_WIDTH ** -0.5),
        "out_norm_g": gain(ks[14], (DEPTH, D_MIX)),
        "mix_w_out": nrm(ks[15], (DEPTH, D_MIX, D_MODEL), D_MIX ** -0.5),
        "norm_ffn2_g": gain(ks[16], (DEPTH, D_MODEL)),
        "ffn2_w_gu": nrm(ks[17], (DEPTH, D_MODEL, 2 * D_FF), D_MODEL ** -0.5),
        "ffn2_w_down": nrm(ks[18], (DEPTH, D_FF, D_MODEL), D_FF ** -0.5),
        "final_norm_g": gain(ks[19], (D_MODEL,)),
    }


def reference(x, c, ada_w, ada_b, norm_ffn1_g, ffn1_w_gu, ffn1_w_down, norm_mix_g,
              mix_w_in, sgu_ln_g, sgu_ln_b, sgu_w_s, sgu_b, conv_w, out_norm_g,
              mix_w_out, norm_ffn2_g, ffn2_w_gu, ffn2_w_down, final_norm_g):
    c_act = jax.nn.silu(c)
    split_points = [D_A, 2 * D_A, 2 * D_A + D_B, 2 * D_A + 2 * D_B]
    for l in range(DEPTH):
        ada = jnp.einsum('bd,de->be', c_act, ada_w[l]) + ada_b[l]
        (sh1, sc1, g1, sh2, sc2, g2, sh3, sc3, g3) = jnp.split(ada, N_MOD, axis=-1)

        h = modulate(rms_norm(x, norm_ffn1_g[l]), sh1, sc1)
        x = x + 0.5 * g1[:, None, :] * swiglu_ffn(h, ffn1_w_gu[l], ffn1_w_down[l])

        h = modulate(rms_norm(x, norm_mix_g[l]), sh2, sc2)
        proj = jnp.einsum('bsd,de->bse', h, mix_w_in[l])
        u_a, v_a, b_gate, c_gate, xb = jnp.split(proj, split_points, axis=-1)
        y_a = chunked_sgu(jax.nn.gelu(u_a, approximate=False), jax.nn.gelu(v_a, approximate=False),
                          sgu_ln_g[l], sgu_ln_b[l], sgu_w_s[l], sgu_b[l])
        y_b = short_gated_conv(b_gate, c_gate, xb, conv_w[l])
        y = jnp.concatenate([rms_norm(y_a, out_norm_g[l, :D_A]),
                             rms_norm(y_b, out_norm_g[l, D_A:])], axis=-1)
        x = x + g2[:, None, :] * jnp.einsum('bse,ed->bsd', y, mix_w_out[l])

        h = modulate(rms_norm(x, norm_ffn2_g[l]), sh3, sc3)
        x = x + 0.5 * g3[:, None, :] * swiglu_ffn(h, ffn2_w_gu[l], ffn2_w_down[l])
    return rms_norm(x, final_norm_g)
```

```python
import contextlib
import numpy as np
import concourse.bass as bass
import concourse.mybir as mybir
from concourse.bass_utils import run_bass_kernel_spmd

F32 = mybir.dt.float32
BF16 = mybir.dt.bfloat16
AF = mybir.ActivationFunctionType
ALU = mybir.AluOpType
AX = mybir.AxisListType

D = 1024
DC = 8
FF = 2816
FC = 22
TT = 512
NT = 2
PT = 1024
SLOT = 4096
DNL = FC * 128
NSLOT = 4
NGL = 45
EPS = 1e-6
N_CORES = 8
SEQ = 2048
BPC = 4

P_NG = 0
P_FNG = P_NG + 48
P_ONG = P_FNG + 8
P_CW = P_ONG + 16
P_ADAB = P_CW + 24
P_LNGC = P_ADAB + 144
P_LNBC = P_LNGC + 2
P_BS = P_LNBC + 2
NPARAM = P_BS + 1024


class Tracker:
    def __init__(self, nc, es):
        self.nc, self.es = nc, es
        self.engs = {n: dict(ops=[], count=0, waited={}) for n in ("pe", "act", "dve", "pool", "sp")}
        self.lastw = {}
        self.rds = {}
        self.dma_counts = {}
        self.sem_objs = {}
        self.fences = {}

    def sem(self, name):
        if name not in self.sem_objs:
            self.sem_objs[name] = self.es.enter_context(self.nc.semaphore(name))
        return self.sem_objs[name]

    @staticmethod
    def _tag(k):
        return k[0].split("_")[0]

    def fence(self, to_tag, from_tags):
        need = {}
        for k, ev in self.lastw.items():
            if self._tag(k) in from_tags and ev is not None:
                if ev[1] > need.get(ev[0], 0):
                    need[ev[0]] = ev[1]
        for k, d in self.rds.items():
            if self._tag(k) in from_tags:
                for s, v in d.items():
                    if v > need.get(s, 0):
                        need[s] = v
        self.fences[to_tag] = need

    def _waits(self, eng, reads, writes):
        need = {}

        def add(s, v):
            if v > need.get(s, 0):
                need[s] = v

        for k in reads:
            ev = self.lastw.get(k)
            if ev is not None:
                add(*ev)
            f = self.fences.get(self._tag(k))
            if f:
                for s, v in f.items():
                    add(s, v)
        for k in writes:
            ev = self.lastw.get(k)
            if ev is not None:
                add(*ev)
            for s, v in self.rds.get(k, {}).items():
                add(s, v)
            f = self.fences.get(self._tag(k))
            if f:
                for s, v in f.items():
                    add(s, v)
        E = self.engs[eng]
        out = []
        for s, v in need.items():
            if eng == "pe" and s == "pe":
                continue
            if E["waited"].get(s, 0) >= v:
                continue
            E["waited"][s] = v
            out.append((s, v))
        return out

    def _record(self, ev, reads, writes):
        for k in reads:
            d = self.rds.setdefault(k, {})
            if ev[1] > d.get(ev[0], 0):
                d[ev[0]] = ev[1]
        for k in writes:
            self.lastw[k] = ev
            self.rds[k] = {}

    def op(self, eng, fn, reads=(), writes=(), signal=True):
        E = self.engs[eng]
        waits = self._waits(eng, reads, writes)
        if signal:
            E["count"] += 1
            ev = (eng, E["count"])
        else:
            ev = (eng, E["count"] + 1)
        E["ops"].append((waits, fn, (eng, 1) if signal else None))
        self._record(ev, reads, writes)
        return ev

    def dma(self, q, fns, semname, reads=(), writes=()):
        E = self.engs[q]
        waits = self._waits(q, reads, writes)
        c = self.dma_counts.get(semname, 0)
        for i, fn in enumerate(fns):
            c += 16
            E["ops"].append((waits if i == 0 else [], fn, (semname, 16)))
        self.dma_counts[semname] = c
        ev = (semname, c)
        self._record(ev, reads, writes)
        return ev

    def mmg(self, mms, reads, writes):
        def fn(e, mms=mms):
            ins = None
            for (o, l, r, st, sp) in mms:
                ins = e.matmul(o, lhsT=l, rhs=r, start=st, stop=sp)
            return ins
        return self.op("pe", fn, reads, writes)

    def mmg_fine(self, mms, reads_each, reads_common, writes):
        n = len(mms)
        for i, (o, l, r, st, sp) in enumerate(mms):
            fn = lambda e, o=o, l=l, r=r, st=st, sp=sp: e.matmul(o, lhsT=l, rhs=r, start=st, stop=sp)
            self.op("pe", fn, list(reads_common) + list(reads_each[i]), writes, signal=(i == n - 1))

    def replay(self, eng, handle):
        for waits, fn, inc in self.engs[eng]["ops"]:
            for s, v in waits:
                handle.wait_ge(self.sem(s), v)
            ins = fn(handle)
            if inc is not None:
                ins.then_inc(self.sem(inc[0]), inc[1])


_DBG = {}


class Rot:
    def __init__(self, name, aps):
        self.name, self.aps, self.i = name, aps, 0
        self.skip = set()

    def next(self):
        i = self.i
        while i in self.skip:
            i = (i + 1) % len(self.aps)
        self.i = (i + 1) % len(self.aps)
        return self.aps[i], (self.name, i)


def build_program(npass=8, nl=2):
    nc = bass.Bass("TRN2", target_bir_lowering=False)
    dt = nc.dram_tensor
    x_d = dt("x", [npass, 128, DC, PT], F32, kind="ExternalInput").ap()
    y_d = dt("y", [npass, 128, DC, PT], F32, kind="ExternalOutput").ap()
    cT_d = dt("cT", [128, DC, BPC], F32, kind="ExternalInput").ap()
    par_d = dt("params", [128, NPARAM], F32, kind="ExternalInput").ap()
    wm_d = dt("wm", [128, 2, 8, 128], F32, kind="ExternalInput").ap()
    adaw_d = dt("ada_w", [2, D, 9 * D], F32, kind="ExternalInput").ap()
    wgu_d = [dt("ffn1_w_gu", [2, D, 2 * FF], F32, kind="ExternalInput").ap(),
             dt("ffn2_w_gu", [2, D, 2 * FF], F32, kind="ExternalInput").ap()]
    wdn_d = [dt("ffn1_w_down", [2, FF, D], F32, kind="ExternalInput").ap(),
             dt("ffn2_w_down", [2, FF, D], F32, kind="ExternalInput").ap()]
    win_d = dt("mix_w_in", [2, D, 2560], F32, kind="ExternalInput").ap()
    wout_d = dt("mix_w_out", [2, D, D], F32, kind="ExternalInput").ap()
    scr_d = dt("scr", [2 * NGL, 128, SLOT], BF16, kind="Internal").ap()

    with contextlib.ExitStack() as es:
        sb = lambda name, shape, dtp: es.enter_context(nc.sbuf_tensor(name, shape, dtp))
        xt0 = [sb("x_t0a", [128, DC, TT], F32), sb("x_t0b", [128, DC, TT], F32)]
        xt1 = [sb("x_t1a", [128, DC, TT], F32), sb("x_t1b", [128, DC, TT], F32)]
        xt = [xt0, xt1]
        cur = {"p": 0}

        def xs(c, t):
            return xt[t][cur["p"] % 2][:, c, :]
        hT = sb("hT", [128, DC, PT], BF16)
        U = sb("U", [128, FC * PT], BF16)
        ring = [sb(f"ring{i}", [128, SLOT], BF16) for i in range(NSLOT)]
        yn = sb("yn", [128, DC, TT], BF16)
        z = sb("z", [128, 4, TT + 2], F32)
        sqb_t = sb("sqb", [128, 2, TT], BF16)
        ms_t = sb("ms", [128, 2, TT], F32)
        rstd_t = sb("rstd", [128, 2, TT], F32)
        tmpn_t = sb("tmpn", [128, 2, TT], F32)
        sg_t = sb("sg", [128, 2, TT], F32)
        par = sb("par", [128, NPARAM], F32)
        cT = sb("cTs", [128, DC, BPC], F32)
        cact = sb("cact", [128, DC, BPC], F32)
        ada = sb("ada", [128, 2, 72, BPC], F32)
        Asc = sb("Asc", [128, 2, 3, DC, BPC], F32)
        Gsc = sb("Gsc", [128, 2, 3, DC, BPC], F32)
        ones_bf = sb("ones_bf", [128, 128], BF16)
        mhalf = sb("mhalf", [128, 8], F32)
        epsc = sb("epsc", [128, 8], F32)
        wm_bf = sb("wm_bf", [128, 2, 8, 128], BF16)
        halo = sb("halo", [128, 2, 4, 2], F32)
        Ct = sb("Ct", [128, 2, 4, 128], F32)
        ps = [es.enter_context(nc.psum_tensor(f"ps{i}", [128, TT], F32)) for i in range(8)]

        tr = Tracker(nc, es)

        def uf32(off, n):
            return U[:, off:off + 2 * n].bitcast(F32)

        act_v = U[:, :].rearrange("p (j t) -> p j t", t=PT)
        vg_r = Rot("u_vg", [uf32(i * 1024, 512) for i in range(4)])
        vsq = uf32(4096, 512)
        vhat = U[:, 5120:7168].rearrange("p (k f) -> p k f", f=512)
        ug_r = Rot("u_ug", [uf32(7168, 512), uf32(8192, 512)])
        tm_r = Rot("u_tm", [uf32(9216, 512), uf32(10240, 512)])
        yav = uf32(11264, 2048).rearrange("p (c t) -> p c t", t=512)
        ybv = uf32(15360, 2048).rearrange("p (c t) -> p c t", t=512)
        st_r = Rot("u_st", [uf32(19456 + i * 16, 8) for i in range(48)])
        xn_v = uf32(0, 4096).rearrange("p (c t) -> p c t", t=512)
        adast = [uf32(0, 4096).rearrange("p (c e) -> p c e", e=512),
                 uf32(8192, 4096).rearrange("p (c e) -> p c e", e=512)]
        wm_st = uf32(16384, 2048).rearrange("p (l h t) -> p l h t", l=2, h=8)

        sqb_r = Rot("sqb", [sqb_t[:, i, :] for i in range(2)])
        ms_r = Rot("ms", [ms_t[:, i, :] for i in range(2)])
        rstd_r = Rot("rstd", [rstd_t[:, i, :] for i in range(2)])
        tmpn_r = Rot("tmpn", [tmpn_t[:, i, :] for i in range(2)])
        sg_r = Rot("sg", [sg_t[:, i, :] for i in range(2)])
        ps_r = Rot("ps", [p[:, :] for p in ps])

        def pcol(off):
            return par[:, off:off + 1]

        tr.dma("sp", [lambda e: e.dma_start(out=par[:, :], in_=par_d)], "d_par", writes=[("P",)])
        tr.dma("sp", [lambda e: e.dma_start(out=cT[:, :, :], in_=cT_d)], "d_ct", writes=[("cT",)])
        tr.dma("sp", [lambda e: e.dma_start(out=wm_st, in_=wm_d)], "d_wm", writes=[("pg", "wm")])

        tr.op("pool", lambda e: e.memset(ones_bf[:, :], 1.0), writes=[("ones",)])
        tr.op("pool", lambda e: e.memset(epsc[:, :], EPS), writes=[("mh",)])
        tr.op("pool", lambda e: e.memset(mhalf[:, :], -0.5), writes=[("mh",)])
        for l in range(2):
            for h in range(8):
                tr.op("pool", lambda e, l=l, h=h: e.affine_select(
                    out=wm_st[:, l, h, :], in_=wm_st[:, l, h, :], pattern=[[1, 128]],
                    compare_op=ALU.is_ge, fill=0.0, base=0, channel_multiplier=-1),
                    reads=[("pg", "wm")], writes=[("pg", "wm")])
        tr.op("pool", lambda e: e.tensor_copy(out=wm_bf[:, :, :, :], in_=wm_st),
              reads=[("pg", "wm")], writes=[("wmbf",)])
        tr.op("pool", lambda e: e.memset(halo[:, :, :, :], 0.0), writes=[("halo", 0), ("halo", 1)])

        for l in range(2):
            cb_ap, cb_key = ps_r.next()
            mms = []
            for fc in range(4):
                for hh in range(2):
                    mms.append((cb_ap[hh * 64:(hh + 1) * 64, fc * 128:(fc + 1) * 128], ones_bf[:, 0:64],
                                wm_bf[:, l, 2 * fc + hh, :], True, True))
            tr.mmg(mms, reads=[("wmbf",), ("ones",)], writes=[cb_key])
            tr.op("dve", lambda e, l=l, cb_ap=cb_ap: e.scalar_tensor_tensor(
                out=Ct[:, l, :, :].rearrange("p f t -> p (f t)"), in0=cb_ap, scalar=pcol(P_LNBC + l),
                in1=par[:, P_BS + l * 512:P_BS + (l + 1) * 512], op0=ALU.mult, op1=ALU.add),
                reads=[cb_key, ("P",)], writes=[("Ct",)])

        tr.op("act", lambda e: e.activation(out=cact[:, :, :], in_=cT[:, :, :], func=AF.Silu),
              reads=[("cT",)], writes=[("cact",)])

        for l in range(nl):
            aps_ap, aps_key = ps_r.next()
            ps_r.skip.add(aps_key[1])
            wv = adaw_d[l].rearrange("(c p) e -> p c e", p=128)
            for mg in range(18):
                st_i = mg % 2
                tr.dma("sp", [lambda e, st_i=st_i, mg=mg, wv=wv: e.dma_start(
                    out=adast[st_i], in_=wv[:, :, mg * 512:(mg + 1) * 512])],
                    f"d_ada{st_i}", writes=[("pg", "ada", st_i)])
                mms = []
                for mm in range(4):
                    m = mg * 4 + mm
                    for c in range(DC):
                        mms.append((aps_ap[:, m * 4:(m + 1) * 4], adast[st_i][:, c, mm * 128:(mm + 1) * 128],
                                    cact[:, c, :], c == 0, c == DC - 1))
                tr.mmg(mms, reads=[("pg", "ada", st_i), ("cact",)], writes=[aps_key])
            ps_r.skip.discard(aps_key[1])
            tr.op("dve", lambda e, l=l, aps_ap=aps_ap: e.tensor_tensor(
                out=ada[:, l, :, :], in0=aps_ap[:, 0:288].rearrange("p (m b) -> p m b", b=BPC),
                in1=par[:, P_ADAB + l * 72:P_ADAB + (l + 1) * 72].unsqueeze(2).broadcast_to([128, 72, BPC]),
                op=ALU.add), reads=[aps_key, ("P",)], writes=[("ada",)])
            for s in range(3):
                tr.op("dve", lambda e, l=l, s=s: e.tensor_scalar(
                    out=Asc[:, l, s, :, :], in0=ada[:, l, (3 * s + 1) * 8:(3 * s + 2) * 8, :],
                    scalar1=1.0, scalar2=None, op0=ALU.add), reads=[("ada",)], writes=[("A",)])
                tr.op("dve", lambda e, l=l, s=s: e.tensor_tensor(
                    out=Asc[:, l, s, :, :], in0=Asc[:, l, s, :, :],
                    in1=par[:, P_NG + (l * 3 + s) * 8:P_NG + (l * 3 + s + 1) * 8].unsqueeze(2).broadcast_to([128, DC, BPC]),
                    op=ALU.mult), reads=[("A",), ("P",)], writes=[("A",)])
                tr.op("dve", lambda e, l=l, s=s: e.tensor_scalar(
                    out=Gsc[:, l, s, :, :], in0=ada[:, l, (3 * s + 2) * 8:(3 * s + 3) * 8, :],
                    scalar1=(1.0 if s == 1 else 0.5), scalar2=None, op0=ALU.mult),
                    reads=[("ada",)], writes=[("G",)])

        CONSTS = [("P",), ("A",), ("G",), ("ada",), ("ones",), ("mh",), ("ident",), ("wmbf",)]

        def group_list(l):
            gl = []
            for gg in range(11):
                gl.append(("gu", 0, gg))
            for mg in range(8):
                gl.append(("dn", 0, mg))
            gl += [("mi", 1, 0), ("mi", 3, 0), ("mi", 4, 0), ("mi", 2, 0), ("mi", 0, 0),
                   ("mi", 1, 1), ("mi", 3, 1), ("mo", 0, 0), ("mo", 1, 0),
                   ("mi", 4, 1), ("mi", 2, 1), ("mi", 0, 1), ("mo", 0, 1), ("mo", 1, 1)]
            for gg in range(11):
                gl.append(("gu", 1, gg))
            for mg in range(8):
                gl.append(("dn", 1, mg))
            return gl

        def gid_of(l, g):
            kind, a, b = g
            base = l * NGL
            if kind == "gu":
                return base + (0 if a == 0 else 26) + b
            if kind == "dn":
                return base + (11 if a == 0 else 37) + b
            if kind == "mi":
                return base + 19 + a
            return base + 24 + a

        def group_len(g):
            return DNL if g[0] == "dn" else 4096

        seq = []
        for p in range(npass):
            for l in range(nl):
                for g in group_list(l):
                    seq.append((p, l, g))
        state = dict(next_load=0, cur=0)
        converted = set()

        def emit_load(n):
            p, l, g = seq[n]
            s = n % NSLOT
            gid = gid_of(l, g)
            L = group_len(g)
            slot = ring[s]
            kind, a, b = g
            if gid not in converted:
                converted.add(gid)
                fns = []
                if kind == "gu":
                    wv = wgu_d[a][l].rearrange("(c p) e -> p c e", p=128)
                    sv = slot[:, 0:4096].rearrange("p (c w) -> p c w", w=512)
                    fns.append(lambda e, wv=wv, sv=sv, b=b: e.dma_start(out=sv[:, :, 0:256], in_=wv[:, :, b * 256:(b + 1) * 256]))
                    fns.append(lambda e, wv=wv, sv=sv, b=b: e.dma_start(out=sv[:, :, 256:512], in_=wv[:, :, FF + b * 256:FF + (b + 1) * 256]))
                elif kind == "dn":
                    wv = wdn_d[a][l].rearrange("(k p) d -> p k d", p=128)
                    sv = slot[:, 0:DNL].rearrange("p (k w) -> p k w", w=128)
                    fns.append(lambda e, wv=wv, sv=sv, b=b: e.dma_start(out=sv, in_=wv[:, :, b * 128:(b + 1) * 128]))
                elif kind == "mi":
                    wv = win_d[l].rearrange("(c p) e -> p c e", p=128)
                    sv = slot[:, 0:4096].rearrange("p (c w) -> p c w", w=512)
                    fns.append(lambda e, wv=wv, sv=sv, a=a: e.dma_start(out=sv, in_=wv[:, :, a * 512:(a + 1) * 512]))
                else:
                    wv = wout_d[l].rearrange("(k p) d -> p k d", p=128)
                    sv = slot[:, 0:4096].rearrange("p (k w) -> p k w", w=512)
                    fns.append(lambda e, wv=wv, sv=sv, a=a: e.dma_start(out=sv, in_=wv[:, :, a * 512:(a + 1) * 512]))
                tr.dma("pool", fns, f"d_ringc{s}", writes=[("r", s)])
                tr.dma("sp", [lambda e, slot=slot, gid=gid, L=L: e.dma_start(out=scr_d[gid][:, 0:L], in_=slot[:, 0:L])],
                       f"d_wb{s}", reads=[("r", s)], writes=[("scr", gid)])
            else:
                tr.dma("sp", [lambda e, slot=slot, gid=gid, L=L: e.dma_start(out=slot[:, 0:L], in_=scr_d[gid][:, 0:L])],
                       f"d_ring{s}", reads=[("scr", gid)], writes=[("r", s)])

        def acquire(expect, hold=0):
            n = state["cur"]
            assert seq[n] == expect, (seq[n], expect)
            while state["next_load"] < len(seq) and state["next_load"] <= n - hold + NSLOT - 1:
                emit_load(state["next_load"])
                state["next_load"] += 1
            state["cur"] += 1
            s = n % NSLOT
            return ring[s], ("r", s)

        def xkeys(cs, t):
            bid = t * 2 + cur["p"] % 2
            return [("x", bid, c) for c in cs]

        def tsl(t):
            return slice(t * TT, (t + 1) * TT)

        def rstd_from_ss(ss_ap, ss_key, inv_n, r_dst=None):
            m_ap, m_key = ms_r.next()
            tr.op("act", lambda e: e.activation(out=m_ap, in_=ss_ap, func=AF.Ln, bias=epsc[:, 0:1], scale=inv_n),
                  reads=[ss_key, ("mh",)], writes=[m_key])
            r_ap, r_key = r_dst if r_dst is not None else rstd_r.next()
            tr.op("act", lambda e: e.activation(out=r_ap, in_=m_ap, func=AF.Exp, scale=-0.5),
                  reads=[m_key], writes=[r_key])
            return r_ap, r_key

        def stats_rstd(src_fn, nchunks, src_keys_fn, inv_n):
            ss_ap, ss_key = ps_r.next()
            for c in range(nchunks):
                q_ap, q_key = sqb_r.next()
                tr.op("act", lambda e, q_ap=q_ap, src=src_fn(c): e.activation(out=q_ap, in_=src, func=AF.Square),
                      reads=src_keys_fn(c), writes=[q_key])
                tr.op("pe", lambda e, q_ap=q_ap, c=c, ss_ap=ss_ap: e.matmul(
                    ss_ap, lhsT=ones_bf[:, :], rhs=q_ap, start=(c == 0), stop=(c == nchunks - 1)),
                    reads=[q_key, ("ones",)], writes=[ss_key])
            return rstd_from_ss(ss_ap, ss_key, inv_n)

        def apply_mod(l, s, t, b, r_ap, r_key, pp=None):
            pp = cur["p"] if pp is None else pp
            for c in range(DC):
                tm_ap, tm_key = tmpn_r.next()
                tr.op("dve", lambda e, c=c, tm_ap=tm_ap, xa=xt[t][pp % 2][:, c, :]: e.scalar_tensor_tensor(
                    out=tm_ap, in0=xa, scalar=Asc[:, l, s, c, b:b + 1], in1=r_ap,
                    op0=ALU.mult, op1=ALU.mult), reads=[("x", t * 2 + pp % 2, c), r_key, ("A",)], writes=[tm_key])
                m = 3 * s * 8 + c
                tr.op("act", lambda e, c=c, tm_ap=tm_ap, m=m: e.activation(
                    out=hT[:, c, tsl(t)], in_=tm_ap, func=AF.Identity, bias=ada[:, l, m, b:b + 1], scale=1.0),
                    reads=[tm_key, ("ada",)], writes=[("h", c, t)])

        def norm_mod(l, s, t, b):
            r_ap, r_key = stats_rstd(lambda c: xs(c, t), DC, lambda c: xkeys([c], t), 1.0 / D)
            apply_mod(l, s, t, b, r_ap, r_key)

        def next_norm(l, s):
            if s < 2:
                return (l, s + 1)
            return (l + 1, 0) if l + 1 < nl else None

        class NormAcc:
            def __init__(self, t, on_done, src=None, defer=False):
                self.t, self.on_done, self.n = t, on_done, 0
                self.src, self.defer = src, defer
                self.ss_ap, self.ss_key = ps_r.next()
                ps_r.skip.add(self.ss_key[1])

            def add(self, c):
                t, n, ss_ap, ss_key = self.t, self.n, self.ss_ap, self.ss_key
                q_ap, q_key = sqb_r.next()
                xa, xk = (xs(c, t), xkeys([c], t)) if self.src is None else self.src(c)
                tr.op("act", lambda e: e.activation(out=q_ap, in_=xa, func=AF.Square),
                      reads=xk, writes=[q_key])
                tr.op("pe", lambda e: e.matmul(ss_ap, lhsT=ones_bf[:, :], rhs=q_ap, start=(n == 0), stop=(n == DC - 1)),
                      reads=[q_key, ("ones",)], writes=[ss_key])
                self.n += 1
                if self.n == DC and not self.defer:
                    self.finish()

            def finish(self):
                r_ap, r_key = rstd_from_ss(self.ss_ap, self.ss_key, 1.0 / D)
                ps_r.skip.discard(self.ss_key[1])
                self.on_done(self.t, r_ap, r_key)

        class Lagged:
            def __init__(self, accs):
                self.accs, self.pending, self.deferred = accs, None, []

            def step(self, m, t):
                for fn in self.deferred:
                    fn()
                self.deferred = []
                if self.pending is not None:
                    self.accs[self.pending[1]].add(self.pending[0])
                self.pending = (m, t)

            def flush(self):
                for fn in self.deferred:
                    fn()
                self.deferred = []
                if self.pending is not None:
                    self.accs[self.pending[1]].add(self.pending[0])
                    self.pending = None
                for fn in self.deferred:
                    fn()
                self.deferred = []

        def ffn(p, l, f, b, pre_normed):
            s = 0 if f == 0 else 2
            tr.fence("a", ("u", "pg"))
            if not pre_normed:
                for t in range(NT):
                    norm_mod(l, s, t, b)
            for gg in range(11):
                slot, rkey = acquire((p, l, ("gu", f, gg)))
                sv = slot[:, 0:4096].rearrange("p (c w) -> p c w", w=512)
                for t in range(NT):
                    for jj in range(2):
                        j = gg * 2 + jj
                        g_ap, g_key = ps_r.next()
                        u_ap, u_key = ps_r.next()
                        hke = [[("h", c, t)] for c in range(DC)]
                        tr.mmg_fine([(g_ap, sv[:, c, jj * 128:(jj + 1) * 128], hT[:, c, tsl(t)], c == 0, c == DC - 1)
                                     for c in range(DC)], hke, [rkey], [g_key])
                        tr.mmg_fine([(u_ap, sv[:, c, 256 + jj * 128:256 + (jj + 1) * 128], hT[:, c, tsl(t)], c == 0, c == DC - 1)
                                     for c in range(DC)], hke, [rkey], [u_key])
                        sg_ap, sg_key = sg_r.next()
                        tr.op("act", lambda e, sg_ap=sg_ap, g_ap=g_ap: e.activation(out=sg_ap, in_=g_ap, func=AF.Silu),
                              reads=[g_key], writes=[sg_key])
                        tr.op("dve", lambda e, sg_ap=sg_ap, u_ap=u_ap, j=j, t=t: e.tensor_tensor(
                            out=act_v[:, j, tsl(t)], in0=sg_ap, in1=u_ap, op=ALU.mult),
                            reads=[sg_key, u_key], writes=[("a", j, t)])
            nxt = next_norm(l, s)
            lag = Lagged(None)
            if nxt is not None:
                done = lambda t, r_ap, r_key: apply_mod(nxt[0], nxt[1], t, b, r_ap, r_key)
            else:
                done = lambda t, r_ap, r_key: lag.deferred.append(lambda: final_tile(p, t, r_ap, r_key))
            lag.accs = {t: NormAcc(t, done) for t in range(NT)}
            extra = []
            if nxt is None and p + 1 < npass:
                for t in range(NT):
                    nb = t * 2 + (p + 1) % 2
                    acc = NormAcc(t, lambda t_, r_ap, r_key: apply_mod(0, 0, t_, (p + 1) // 2, r_ap, r_key, pp=p + 1),
                                  src=lambda c, t=t, nb=nb: (xt[t][(p + 1) % 2][:, c, :], [("x", nb, c)]))
                    next_acc[t] = acc
                    extra += [(acc, c) for c in range(DC)]

            def extra_step():
                if extra:
                    acc, c = extra.pop(0)
                    acc.add(c)
            def down_step(m, t, sv, rkey):
                o_ap, o_key = ps_r.next()
                tr.mmg_fine([(o_ap, sv[:, k, :], act_v[:, k, tsl(t)], k == 0, k == FC - 1)
                             for k in range(FC)], [[("a", k, t)] for k in range(FC)], [rkey], [o_key])
                tr.op("dve", lambda e, o_ap=o_ap, m=m, xa=xs(m, t): e.scalar_tensor_tensor(
                    out=xa, in0=o_ap, scalar=Gsc[:, l, s, m, b:b + 1], in1=xa,
                    op0=ALU.mult, op1=ALU.add), reads=[o_key, ("G",)] + xkeys([m], t), writes=xkeys([m], t))
                lag.step(m, t)
                extra_step()

            HOLD = 2
            for m in range(8 - HOLD):
                slot, rkey = acquire((p, l, ("dn", f, m)))
                sv = slot[:, 0:DNL].rearrange("p (k w) -> p k w", w=128)
                for t in range(NT):
                    down_step(m, t, sv, rkey)
            last = []
            for i, m in enumerate(range(8 - HOLD, 8)):
                slot, rkey = acquire((p, l, ("dn", f, m)), hold=i)
                last.append((m, slot[:, 0:DNL].rearrange("p (k w) -> p k w", w=128), rkey))
            for t in range(NT):
                for m, sv, rkey in last:
                    down_step(m, t, sv, rkey)
            lag.flush()

        def group_norm(l, grp, ysrc, ykey):
            r_ap, r_key = stats_rstd(lambda c: ysrc[:, c, :], 4, lambda c: [(ykey, c)], 1.0 / 512)
            for fc in range(4):
                cc = grp * 4 + fc
                tr.op("dve", lambda e, fc=fc, cc=cc: e.scalar_tensor_tensor(
                    out=yn[:, cc, :], in0=ysrc[:, fc, :], scalar=pcol(P_ONG + l * 8 + cc), in1=r_ap,
                    op0=ALU.mult, op1=ALU.mult), reads=[(ykey, fc), r_key, ("P",)], writes=[("yn", cc)])

        def mixer_tile(p, l, t, b, half):
            hk = [("h", c, t) for c in range(DC)]
            zk = [("z", fc) for fc in range(4)]

            def proj_group(gi, fc, sv, rkey):
                o_ap, o_key = ps_r.next()
                tr.mmg_fine([(o_ap, sv[:, c, fc * 128:(fc + 1) * 128], hT[:, c, tsl(t)], c == 0, c == DC - 1)
                             for c in range(DC)], [[k_] for k_ in hk], [rkey], [o_key])
                return o_ap, o_key

            slot, rkey = acquire((p, l, ("mi", 1, t)))
            sv = slot[:, 0:4096].rearrange("p (c w) -> p c w", w=512)
            vinfo = []
            for tk in range(4):
                v_ap, v_key = ps_r.next()
                tok = slice(t * TT + tk * 128, t * TT + (tk + 1) * 128)
                tr.mmg_fine([(v_ap, hT[:, c, tok], sv[:, c, :], c == 0, c == DC - 1) for c in range(DC)],
                            [[k_] for k_ in hk], [rkey], [v_key])
                vg, vg_key = vg_r.next()
                tr.op("act", lambda e, vg=vg, v_ap=v_ap: e.activation(out=vg, in_=v_ap, func=AF.Gelu),
                      reads=[v_key], writes=[vg_key])
                vg3 = vg.rearrange("p (h d) -> p h d", d=64)
                s1, s1k = st_r.next()
                s2, s2k = st_r.next()
                mean, mk = st_r.next()
                msq, msk = st_r.next()
                var, vk = st_r.next()
                rs, rk = st_r.next()
                tr.op("dve", lambda e, s1=s1, vg3=vg3: e.tensor_reduce(out=s1, in_=vg3, axis=AX.X, op=ALU.add),
                      reads=[vg_key], writes=[s1k])
                tr.op("act", lambda e, vg=vg: e.activation(out=vsq, in_=vg, func=AF.Square),
                      reads=[vg_key], writes=[("u_vsq",)])
                tr.op("dve", lambda e, s2=s2: e.tensor_reduce(
                    out=s2, in_=vsq.rearrange("p (h d) -> p h d", d=64), axis=AX.X, op=ALU.add),
                    reads=[("u_vsq",)], writes=[s2k])
                tr.op("dve", lambda e, mean=mean, s1=s1: e.tensor_scalar(
                    out=mean, in0=s1, scalar1=1.0 / 64, scalar2=None, op0=ALU.mult), reads=[s1k], writes=[mk])
                tr.op("dve", lambda e, mean=mean, msq=msq: e.tensor_tensor(out=msq, in0=mean, in1=mean, op=ALU.mult),
                      reads=[mk], writes=[msk])
                tr.op("dve", lambda e, var=var, s2=s2, msq=msq: e.scalar_tensor_tensor(
                    out=var, in0=s2, scalar=1.0 / 64, in1=msq, op0=ALU.mult, op1=ALU.subtract),
                    reads=[s2k, msk], writes=[vk])
                tr.op("dve", lambda e, var=var: e.tensor_scalar(
                    out=var, in0=var, scalar1=EPS, scalar2=None, op0=ALU.add), reads=[vk], writes=[vk])
                tr.op("pool", lambda e, rs=rs, var=var: e.tensor_tensor(out=rs, in0=var, in1=mhalf[:, 0:8], op=ALU.pow),
                      reads=[vk, ("mh",)], writes=[rk])
                vinfo.append((vg, vg_key, vg3, mean, mk, rs, rk))
            yield "v1"
            first = (half == 0 and t == 0)
            if first:
                tr.op("pool", lambda e: e.memset(z[:, :, 0:2], 0.0), writes=zk)
            elif t == 0:
                tr.op("dve", lambda e: e.tensor_copy(out=z[:, :, 0:2], in_=halo[:, l, :, :]),
                      reads=[("halo", l)], writes=zk)
            else:
                tr.op("dve", lambda e: e.tensor_copy(out=z[:, :, 0:2], in_=z[:, :, TT:TT + 2]),
                      reads=zk, writes=zk)
            slot, rkey = acquire((p, l, ("mi", 3, t)))
            sv = slot[:, 0:4096].rearrange("p (c w) -> p c w", w=512)
            for fc in range(4):
                c_ap, c_key = proj_group(3, fc, sv, rkey)
                tr.op("act", lambda e, c_ap=c_ap, fc=fc: e.activation(out=z[:, fc, 2:TT + 2], in_=c_ap, func=AF.Copy),
                      reads=[c_key], writes=[("z", fc)])
            for tk in range(4):
                vg, vg_key, vg3, mean, mk, rs, rk = vinfo[tk]
                tr.op("dve", lambda e, vg3=vg3, mean=mean: e.tensor_tensor(
                    out=vg3, in0=vg3, in1=mean.unsqueeze(2).broadcast_to([128, 8, 64]), op=ALU.subtract),
                    reads=[vg_key, mk], writes=[vg_key])
                tr.op("dve", lambda e, vg3=vg3, rs=rs, tk=tk: e.tensor_tensor(
                    out=vhat[:, tk, :].rearrange("p (h d) -> p h d", d=64), in0=vg3,
                    in1=rs.unsqueeze(2).broadcast_to([128, 8, 64]), op=ALU.mult),
                    reads=[vg_key, rk], writes=[("u_vhat", tk)])
            yield "c_v2"
            slot, rkey = acquire((p, l, ("mi", 4, t)))
            sv = slot[:, 0:4096].rearrange("p (c w) -> p c w", w=512)
            for fc in range(4):
                xb_ap, xb_key = proj_group(4, fc, sv, rkey)
                tr.op("dve", lambda e, xb_ap=xb_ap, fc=fc: e.tensor_tensor(
                    out=z[:, fc, 2:TT + 2], in0=z[:, fc, 2:TT + 2], in1=xb_ap, op=ALU.mult),
                    reads=[("z", fc), xb_key], writes=[("z", fc)])
            if t == NT - 1:
                tr.op("dve", lambda e: e.tensor_copy(out=halo[:, l, :, :], in_=z[:, :, TT:TT + 2]),
                      reads=zk, writes=[("halo", l)])
            for fc in range(4):
                cw = lambda k, fc=fc: pcol(P_CW + (l * 3 + k) * 4 + fc)
                tr.op("dve", lambda e, fc=fc, cw=cw: e.tensor_scalar(
                    out=ybv[:, fc, :], in0=z[:, fc, 2:TT + 2], scalar1=cw(2), scalar2=None, op0=ALU.mult),
                    reads=[("z", fc), ("P",)], writes=[("u_yb", fc)])
                tr.op("dve", lambda e, fc=fc, cw=cw: e.scalar_tensor_tensor(
                    out=ybv[:, fc, :], in0=z[:, fc, 1:TT + 1], scalar=cw(1), in1=ybv[:, fc, :], op0=ALU.mult, op1=ALU.add),
                    reads=[("z", fc), ("P",), ("u_yb", fc)], writes=[("u_yb", fc)])
                tr.op("dve", lambda e, fc=fc, cw=cw: e.scalar_tensor_tensor(
                    out=ybv[:, fc, :], in0=z[:, fc, 0:TT], scalar=cw(0), in1=ybv[:, fc, :], op0=ALU.mult, op1=ALU.add),
                    reads=[("z", fc), ("P",), ("u_yb", fc)], writes=[("u_yb", fc)])
            slot, rkey = acquire((p, l, ("mi", 2, t)))
            sv = slot[:, 0:4096].rearrange("p (c w) -> p c w", w=512)
            for fc in range(4):
                b_ap, b_key = proj_group(2, fc, sv, rkey)
                tr.op("dve", lambda e, b_ap=b_ap, fc=fc: e.tensor_tensor(
                    out=ybv[:, fc, :], in0=ybv[:, fc, :], in1=b_ap, op=ALU.mult),
                    reads=[("u_yb", fc), b_key], writes=[("u_yb", fc)])
            yield "convB"
            slot, rkey = acquire((p, l, ("mi", 0, t)))
            sv = slot[:, 0:4096].rearrange("p (c w) -> p c w", w=512)
            for fc in range(4):
                u_ap, u_key = proj_group(0, fc, sv, rkey)
                ug, ug_key = ug_r.next()
                tr.op("act", lambda e, ug=ug, u_ap=u_ap: e.activation(out=ug, in_=u_ap, func=AF.Gelu),
                      reads=[u_key], writes=[ug_key])
                m_ap, m_key = ps_r.next()
                mms = []
                for tk in range(4):
                    for hh in range(2):
                        mms.append((m_ap[hh * 64:(hh + 1) * 64, tk * 128:(tk + 1) * 128],
                                    vhat[:, tk, fc * 128 + hh * 64:fc * 128 + (hh + 1) * 64],
                                    wm_bf[:, l, 2 * fc + hh, :], True, True))
                tr.mmg(mms, reads=[("u_vhat", tk) for tk in range(4)] + [("wmbf",)], writes=[m_key])
                tm, tm_key = tm_r.next()
                tr.op("dve", lambda e, tm=tm, m_ap=m_ap, fc=fc: e.scalar_tensor_tensor(
                    out=tm.rearrange("p (k t) -> p k t", t=128), in0=m_ap.rearrange("p (k t) -> p k t", t=128),
                    scalar=pcol(P_LNGC + l), in1=Ct[:, l, fc, :].unsqueeze(1).broadcast_to([128, 4, 128]),
                    op0=ALU.mult, op1=ALU.add), reads=[m_key, ("P",), ("Ct",)], writes=[tm_key])
                tr.op("dve", lambda e, tm=tm, ug=ug, fc=fc: e.tensor_tensor(
                    out=yav[:, fc, :], in0=tm, in1=ug, op=ALU.mult), reads=[tm_key, ug_key], writes=[("u_ya", fc)])
            yield "uS"
            group_norm(l, 1, ybv, "u_yb")
            group_norm(l, 0, yav, "u_ya")
            yield "normA"
            lag = Lagged({t: NormAcc(t, lambda t_, r_ap, r_key: apply_mod(l, 2, t_, b, r_ap, r_key))})
            for og in range(2):
                slot, rkey = acquire((p, l, ("mo", og, t)))
                sv = slot[:, 0:4096].rearrange("p (k w) -> p k w", w=512)
                for mm in range(4):
                    m = og * 4 + mm
                    o_ap, o_key = ps_r.next()
                    tr.mmg_fine([(o_ap, sv[:, k, mm * 128:(mm + 1) * 128], yn[:, k, :], k == 0, k == DC - 1)
                                 for k in range(DC)], [[("yn", k)] for k in range(DC)], [rkey], [o_key])
                    tr.op("dve", lambda e, o_ap=o_ap, m=m, xa=xs(m, t): e.scalar_tensor_tensor(
                        out=xa, in0=o_ap, scalar=Gsc[:, l, 1, m, b:b + 1], in1=xa,
                        op0=ALU.mult, op1=ALU.add), reads=[o_key, ("G",)] + xkeys([m], t), writes=xkeys([m], t))
                    lag.step(m, t)
            lag.flush()

        next_acc = {}

        def load_tile(p, t):
            buf = xt[t][p % 2]
            bid = t * 2 + p % 2
            tr.dma("sp", [lambda e, buf=buf, t=t, p=p: e.dma_start(out=buf[:, :, :], in_=x_d[p][:, :, t * TT:(t + 1) * TT])],
                   f"d_x{bid}", writes=[("x", bid, c) for c in range(DC)])

        def final_tile(p, t, r_ap, r_key):
            for c in range(DC):
                tr.op("dve", lambda e, c=c, xa=xs(c, t): e.scalar_tensor_tensor(
                    out=xa, in0=xa, scalar=pcol(P_FNG + c), in1=r_ap,
                    op0=ALU.mult, op1=ALU.mult), reads=xkeys([c], t) + [r_key, ("P",)], writes=xkeys([c], t))
            tr.dma("sp", [lambda e, src_all=xt[t][p % 2][:, :, :], t=t, p=p: e.dma_start(
                out=y_d[p][:, :, t * TT:(t + 1) * TT], in_=src_all)],
                f"d_out{t}", reads=xkeys(range(DC), t), writes=[("yout", t)])

        load_tile(0, 0)
        load_tile(0, 1)
        for p in range(npass):
            cur["p"] = p
            b = p // 2
            half = p % 2
            if p + 1 < npass:
                load_tile(p + 1, 0)
                load_tile(p + 1, 1)
            for t in range(NT):
                if t in next_acc:
                    assert next_acc.pop(t).n == DC
                else:
                    norm_mod(0, 0, t, b)
            for l in range(nl):
                ffn(p, l, 0, b, pre_normed=True)
                tr.fence("u", ("a",))
                g0 = mixer_tile(p, l, 0, b, half)
                g1 = mixer_tile(p, l, 1, b, half)

                def run(g, until):
                    for tag in g:
                        if tag == until:
                            return
                run(g0, "uS")
                run(g1, "v1")
                run(g0, "normA")
                run(g1, "c_v2")
                run(g0, None)
                run(g1, None)
                ffn(p, l, 1, b, pre_normed=True)

        out_sems = [(s, v) for s, v in tr.dma_counts.items() if s.startswith("d_out") or s.startswith("d_wb")]

        for E in tr.engs.values():
            for waits, fn, inc in E["ops"]:
                for s, v in waits:
                    tr.sem(s)
                if inc is not None:
                    tr.sem(inc[0])

        _DBG['tr'] = tr
        block = es.enter_context(nc.Block())

        @block.tensor
        def _(e):
            tr.replay("pe", e)

        @block.scalar
        def _(e):
            tr.replay("act", e)

        @block.vector
        def _(e):
            tr.replay("dve", e)

        @block.gpsimd
        def _(e):
            tr.replay("pool", e)

        @block.sync
        def _(e):
            tr.replay("sp", e)
            for s, v in out_sems:
                e.wait_ge(tr.sem(s), v)
    return nc


def _pack_params(inp):
    par = np.zeros((128, NPARAM), np.float32)

    def fm(v):
        v = np.asarray(v, np.float32)
        n = v.shape[-1] // 128
        lead = v.shape[:-1]
        return np.moveaxis(v.reshape(lead + (n, 128)), -1, 0)

    ng = np.stack([inp["norm_ffn1_g"], inp["norm_mix_g"], inp["norm_ffn2_g"]], axis=1)
    par[:, P_NG:P_NG + 48] = fm(ng).reshape(128, 48)
    par[:, P_FNG:P_FNG + 8] = fm(inp["final_norm_g"]).reshape(128, 8)
    par[:, P_ONG:P_ONG + 16] = fm(inp["out_norm_g"]).reshape(128, 16)
    par[:, P_CW:P_CW + 24] = fm(inp["conv_w"]).reshape(128, 24)
    par[:, P_ADAB:P_ADAB + 144] = fm(inp["ada_b"]).reshape(128, 144)
    lng = np.asarray(inp["sgu_ln_g"], np.float32)
    lnb = np.asarray(inp["sgu_ln_b"], np.float32)
    par[:, P_LNGC:P_LNGC + 2] = np.tile(lng, (1, 2)).T
    par[:, P_LNBC:P_LNBC + 2] = np.tile(lnb, (1, 2)).T
    bsv = np.asarray(inp["sgu_b"], np.float32)
    bs = np.repeat(bsv.reshape(2, 4, 2, 1, 128), 64, axis=3).reshape(2, 4, 128, 128)
    par[:, P_BS:P_BS + 1024] = np.transpose(bs, (2, 0, 1, 3)).reshape(128, 1024)
    return par


def make_in_maps(inp, n_cores=N_CORES, npass=8):
    par = _pack_params(inp)
    wm = np.ascontiguousarray(np.transpose(np.asarray(inp["sgu_w_s"], np.float32), (3, 0, 1, 2)))
    shared = {
        "params": par, "wm": wm,
        "ada_w": np.ascontiguousarray(inp["ada_w"], dtype=np.float32),
        "ffn1_w_gu": np.ascontiguousarray(inp["ffn1_w_gu"], dtype=np.float32),
        "ffn2_w_gu": np.ascontiguousarray(inp["ffn2_w_gu"], dtype=np.float32),
        "ffn1_w_down": np.ascontiguousarray(inp["ffn1_w_down"], dtype=np.float32),
        "ffn2_w_down": np.ascontiguousarray(inp["ffn2_w_down"], dtype=np.float32),
        "mix_w_in": np.ascontiguousarray(inp["mix_w_in"], dtype=np.float32),
        "mix_w_out": np.ascontiguousarray(inp["mix_w_out"], dtype=np.float32),
    }
    x = np.asarray(inp["x"], np.float32)
    c = np.asarray(inp["c"], np.float32)
    maps = []
    for i in range(n_cores):
        xi = x[BPC * i:BPC * (i + 1)].reshape(BPC * SEQ, D)[:npass * PT]
        xi = np.ascontiguousarray(xi.reshape(npass, PT, DC, 128).transpose(0, 3, 2, 1))
        ci = c[BPC * i:BPC * (i + 1)]
        cT = np.ascontiguousarray(np.transpose(ci.T.reshape(DC, 128, BPC), (1, 0, 2)))
        m = dict(shared)
        m["x"] = xi
        m["cT"] = cT
        maps.append(m)
    return maps


def unpack_y(y, npass=8):
    return np.asarray(y, np.float32).reshape(npass, 128, DC, PT).transpose(0, 3, 2, 1).reshape(npass * PT, D)


_NC_CACHE = {}


def kernel(**inputs):
    if "full" not in _NC_CACHE:
        _NC_CACHE["full"] = build_program(8, 2)
    nc = _NC_CACHE["full"]
    in_maps = make_in_maps(inputs)
    res = run_bass_kernel_spmd(nc, in_maps, core_ids=list(range(N_CORES)))
    outs = [unpack_y(r["y"]).reshape(BPC, SEQ, D) for r in res.results]
    return np.concatenate(outs, axis=0)
```

```python
import contextlib
import numpy as np
import concourse.bass as bass
import concourse.mybir as mybir
from concourse.bass_utils import run_bass_kernel_spmd

F32 = mybir.dt.float32
BF16 = mybir.dt.bfloat16
AF = mybir.ActivationFunctionType
ALU = mybir.AluOpType
AX = mybir.AxisListType

D = 1024
DC = 8
FF = 2816
FC = 22
TT = 512
NT = 2
PT = 1024
SLOT = 4096
DNL = FC * 128
NSLOT = 4
NGL = 45
EPS = 1e-6
N_CORES = 8
SEQ = 2048
BPC = 4

P_NG = 0
P_FNG = P_NG + 48
P_ONG = P_FNG + 8
P_CW = P_ONG + 16
P_ADAB = P_CW + 24
P_LNGC = P_ADAB + 144
P_LNBC = P_LNGC + 2
P_BS = P_LNBC + 2
NPARAM = P_BS + 1024


class Tracker:
    def __init__(self, nc, es):
        self.nc, self.es = nc, es
        self.engs = {n: dict(ops=[], count=0, waited={}) for n in ("pe", "act", "dve", "pool", "sp")}
        self.lastw = {}
        self.rds = {}
        self.dma_counts = {}
        self.sem_objs = {}
        self.fences = {}

    def sem(self, name):
        if name not in self.sem_objs:
            self.sem_objs[name] = self.es.enter_context(self.nc.semaphore(name))
        return self.sem_objs[name]

    @staticmethod
    def _tag(k):
        return k[0].split("_")[0]

    def fence(self, to_tag, from_tags):
        need = {}
        for k, ev in self.lastw.items():
            if self._tag(k) in from_tags and ev is not None:
                if ev[1] > need.get(ev[0], 0):
                    need[ev[0]] = ev[1]
        for k, d in self.rds.items():
            if self._tag(k) in from_tags:
                for s, v in d.items():
                    if v > need.get(s, 0):
                        need[s] = v
        self.fences[to_tag] = need

    def _waits(self, eng, reads, writes):
        need = {}

        def add(s, v):
            if v > need.get(s, 0):
                need[s] = v

        for k in reads:
            ev = self.lastw.get(k)
            if ev is not None:
                add(*ev)
            f = self.fences.get(self._tag(k))
            if f:
                for s, v in f.items():
                    add(s, v)
        for k in writes:
            ev = self.lastw.get(k)
            if ev is not None:
                add(*ev)
            for s, v in self.rds.get(k, {}).items():
                add(s, v)
            f = self.fences.get(self._tag(k))
            if f:
                for s, v in f.items():
                    add(s, v)
        E = self.engs[eng]
        out = []
        for s, v in need.items():
            if eng == "pe" and s == "pe":
                continue
            if E["waited"].get(s, 0) >= v:
                continue
            E["waited"][s] = v
            out.append((s, v))
        return out

    def _record(self, ev, reads, writes):
        for k in reads:
            d = self.rds.setdefault(k, {})
            if ev[1] > d.get(ev[0], 0):
                d[ev[0]] = ev[1]
        for k in writes:
            self.lastw[k] = ev
            self.rds[k] = {}

    def op(self, eng, fn, reads=(), writes=(), signal=True):
        E = self.engs[eng]
        waits = self._waits(eng, reads, writes)
        if signal:
            E["count"] += 1
            ev = (eng, E["count"])
        else:
            ev = (eng, E["count"] + 1)
        E["ops"].append((waits, fn, (eng, 1) if signal else None))
        self._record(ev, reads, writes)
        return ev

    def dma(self, q, fns, semname, reads=(), writes=()):
        E = self.engs[q]
        waits = self._waits(q, reads, writes)
        c = self.dma_counts.get(semname, 0)
        for i, fn in enumerate(fns):
            c += 16
            E["ops"].append((waits if i == 0 else [], fn, (semname, 16)))
        self.dma_counts[semname] = c
        ev = (semname, c)
        self._record(ev, reads, writes)
        return ev

    def mmg(self, mms, reads, writes):
        def fn(e, mms=mms):
            ins = None
            for (o, l, r, st, sp) in mms:
                ins = e.matmul(o, lhsT=l, rhs=r, start=st, stop=sp)
            return ins
        return self.op("pe", fn, reads, writes)

    def mmg_fine(self, mms, reads_each, reads_common, writes):
        n = len(mms)
        for i, (o, l, r, st, sp) in enumerate(mms):
            fn = lambda e, o=o, l=l, r=r, st=st, sp=sp: e.matmul(o, lhsT=l, rhs=r, start=st, stop=sp)
            self.op("pe", fn, list(reads_common) + list(reads_each[i]), writes, signal=(i == n - 1))

    def replay(self, eng, handle):
        for waits, fn, inc in self.engs[eng]["ops"]:
            for s, v in waits:
                handle.wait_ge(self.sem(s), v)
            ins = fn(handle)
            if inc is not None:
                ins.then_inc(self.sem(inc[0]), inc[1])


_DBG = {}


class Rot:
    def __init__(self, name, aps):
        self.name, self.aps, self.i = name, aps, 0
        self.skip = set()

    def next(self):
        i = self.i
        while i in self.skip:
            i = (i + 1) % len(self.aps)
        self.i = (i + 1) % len(self.aps)
        return self.aps[i], (self.name, i)


def build_program(npass=8, nl=2):
    nc = bass.Bass("TRN2", target_bir_lowering=False)
    dt = nc.dram_tensor
    x_d = dt("x", [npass, 128, DC, PT], F32, kind="ExternalInput").ap()
    y_d = dt("y", [npass, 128, DC, PT], F32, kind="ExternalOutput").ap()
    cT_d = dt("cT", [128, DC, BPC], F32, kind="ExternalInput").ap()
    par_d = dt("params", [128, NPARAM], F32, kind="ExternalInput").ap()
    wm_d = dt("wm", [128, 2, 8, 128], F32, kind="ExternalInput").ap()
    adaw_d = dt("ada_w", [2, D, 9 * D], F32, kind="ExternalInput").ap()
    wgu_d = [dt("ffn1_w_gu", [2, D, 2 * FF], F32, kind="ExternalInput").ap(),
             dt("ffn2_w_gu", [2, D, 2 * FF], F32, kind="ExternalInput").ap()]
    wdn_d = [dt("ffn1_w_down", [2, FF, D], F32, kind="ExternalInput").ap(),
             dt("ffn2_w_down", [2, FF, D], F32, kind="ExternalInput").ap()]
    win_d = dt("mix_w_in", [2, D, 2560], F32, kind="ExternalInput").ap()
    wout_d = dt("mix_w_out", [2, D, D], F32, kind="ExternalInput").ap()
    scr_d = dt("scr", [2 * NGL, 128, SLOT], BF16, kind="Internal").ap()

    with contextlib.ExitStack() as es:
        sb = lambda name, shape, dtp: es.enter_context(nc.sbuf_tensor(name, shape, dtp))
        xt0 = [sb("x_t0a", [128, DC, TT], F32), sb("x_t0b", [128, DC, TT], F32)]
        xt1 = [sb("x_t1a", [128, DC, TT], F32), sb("x_t1b", [128, DC, TT], F32)]
        xt = [xt0, xt1]
        cur = {"p": 0}

        def xs(c, t):
            return xt[t][cur["p"] % 2][:, c, :]
        hT = sb("hT", [128, DC, PT], BF16)
        U = sb("U", [128, FC * PT], BF16)
        ring = [sb(f"ring{i}", [128, SLOT], BF16) for i in range(NSLOT)]
        yn = sb("yn", [128, DC, TT], BF16)
        z = sb("z", [128, 4, TT + 2], F32)
        sqb_t = sb("sqb", [128, 2, TT], BF16)
        ms_t = sb("ms", [128, 2, TT], F32)
        rstd_t = sb("rstd", [128, 2, TT], F32)
        tmpn_t = sb("tmpn", [128, 2, TT], F32)
        sg_t = sb("sg", [128, 2, TT], F32)
        par = sb("par", [128, NPARAM], F32)
        cT = sb("cTs", [128, DC, BPC], F32)
        cact = sb("cact", [128, DC, BPC], F32)
        ada = sb("ada", [128, 2, 72, BPC], F32)
        Asc = sb("Asc", [128, 2, 3, DC, BPC], F32)
        Gsc = sb("Gsc", [128, 2, 3, DC, BPC], F32)
        ones_bf = sb("ones_bf", [128, 128], BF16)
        mhalf = sb("mhalf", [128, 8], F32)
        epsc = sb("epsc", [128, 8], F32)
        wm_bf = sb("wm_bf", [128, 2, 8, 128], BF16)
        halo = sb("halo", [128, 2, 4, 2], F32)
        Ct = sb("Ct", [128, 2, 4, 128], F32)
        ps = [es.enter_context(nc.psum_tensor(f"ps{i}", [128, TT], F32)) for i in range(8)]

        tr = Tracker(nc, es)

        def uf32(off, n):
            return U[:, off:off + 2 * n].bitcast(F32)

        act_v = U[:, :].rearrange("p (j t) -> p j t", t=PT)
        vg_r = Rot("u_vg", [uf32(i * 1024, 512) for i in range(4)])
        vsq = uf32(4096, 512)
        vhat = U[:, 5120:7168].rearrange("p (k f) -> p k f", f=512)
        ug_r = Rot("u_ug", [uf32(7168, 512), uf32(8192, 512)])
        tm_r = Rot("u_tm", [uf32(9216, 512), uf32(10240, 512)])
        yav = uf32(11264, 2048).rearrange("p (c t) -> p c t", t=512)
        ybv = uf32(15360, 2048).rearrange("p (c t) -> p c t", t=512)
        st_r = Rot("u_st", [uf32(19456 + i * 16, 8) for i in range(48)])
        xn_v = uf32(0, 4096).rearrange("p (c t) -> p c t", t=512)
        adast = [uf32(0, 4096).rearrange("p (c e) -> p c e", e=512),
                 uf32(8192, 4096).rearrange("p (c e) -> p c e", e=512)]
        wm_st = uf32(16384, 2048).rearrange("p (l h t) -> p l h t", l=2, h=8)

        sqb_r = Rot("sqb", [sqb_t[:, i, :] for i in range(2)])
        ms_r = Rot("ms", [ms_t[:, i, :] for i in range(2)])
        rstd_r = Rot("rstd", [rstd_t[:, i, :] for i in range(2)])
        tmpn_r = Rot("tmpn", [tmpn_t[:, i, :] for i in range(2)])
        sg_r = Rot("sg", [sg_t[:, i, :] for i in range(2)])
        ps_r = Rot("ps", [p[:, :] for p in ps])

        def pcol(off):
            return par[:, off:off + 1]

        tr.dma("sp", [lambda e: e.dma_start(out=par[:, :], in_=par_d)], "d_par", writes=[("P",)])
        tr.dma("sp", [lambda e: e.dma_start(out=cT[:, :, :], in_=cT_d)], "d_ct", writes=[("cT",)])
        tr.dma("sp", [lambda e: e.dma_start(out=wm_st, in_=wm_d)], "d_wm", writes=[("pg", "wm")])

        tr.op("pool", lambda e: e.memset(ones_bf[:, :], 1.0), writes=[("ones",)])
        tr.op("pool", lambda e: e.memset(epsc[:, :], EPS), writes=[("mh",)])
        tr.op("pool", lambda e: e.memset(mhalf[:, :], -0.5), writes=[("mh",)])
        for l in range(2):
            for h in range(8):
                tr.op("pool", lambda e, l=l, h=h: e.affine_select(
                    out=wm_st[:, l, h, :], in_=wm_st[:, l, h, :], pattern=[[1, 128]],
                    compare_op=ALU.is_ge, fill=0.0, base=0, channel_multiplier=-1),
                    reads=[("pg", "wm")], writes=[("pg", "wm")])
        tr.op("pool", lambda e: e.tensor_copy(out=wm_bf[:, :, :, :], in_=wm_st),
              reads=[("pg", "wm")], writes=[("wmbf",)])
        tr.op("pool", lambda e: e.memset(halo[:, :, :, :], 0.0), writes=[("halo", 0), ("halo", 1)])

        for l in range(2):
            cb_ap, cb_key = ps_r.next()
            mms = []
            for fc in range(4):
                for hh in range(2):
                    mms.append((cb_ap[hh * 64:(hh + 1) * 64, fc * 128:(fc + 1) * 128], ones_bf[:, 0:64],
                                wm_bf[:, l, 2 * fc + hh, :], True, True))
            tr.mmg(mms, reads=[("wmbf",), ("ones",)], writes=[cb_key])
            tr.op("dve", lambda e, l=l, cb_ap=cb_ap: e.scalar_tensor_tensor(
                out=Ct[:, l, :, :].rearrange("p f t -> p (f t)"), in0=cb_ap, scalar=pcol(P_LNBC + l),
                in1=par[:, P_BS + l * 512:P_BS + (l + 1) * 512], op0=ALU.mult, op1=ALU.add),
                reads=[cb_key, ("P",)], writes=[("Ct",)])

        tr.op("act", lambda e: e.activation(out=cact[:, :, :], in_=cT[:, :, :], func=AF.Silu),
              reads=[("cT",)], writes=[("cact",)])

        for l in range(nl):
            aps_ap, aps_key = ps_r.next()
            ps_r.skip.add(aps_key[1])
            wv = adaw_d[l].rearrange("(c p) e -> p c e", p=128)
            for mg in range(18):
                st_i = mg % 2
                tr.dma("sp", [lambda e, st_i=st_i, mg=mg, wv=wv: e.dma_start(
                    out=adast[st_i], in_=wv[:, :, mg * 512:(mg + 1) * 512])],
                    f"d_ada{st_i}", writes=[("pg", "ada", st_i)])
                mms = []
                for mm in range(4):
                    m = mg * 4 + mm
                    for c in range(DC):
                        mms.append((aps_ap[:, m * 4:(m + 1) * 4], adast[st_i][:, c, mm * 128:(mm + 1) * 128],
                                    cact[:, c, :], c == 0, c == DC - 1))
                tr.mmg(mms, reads=[("pg", "ada", st_i), ("cact",)], writes=[aps_key])
            ps_r.skip.discard(aps_key[1])
            tr.op("dve", lambda e, l=l, aps_ap=aps_ap: e.tensor_tensor(
                out=ada[:, l, :, :], in0=aps_ap[:, 0:288].rearrange("p (m b) -> p m b", b=BPC),
                in1=par[:, P_ADAB + l * 72:P_ADAB + (l + 1) * 72].unsqueeze(2).broadcast_to([128, 72, BPC]),
                op=ALU.add), reads=[aps_key, ("P",)], writes=[("ada",)])
            for s in range(3):
                tr.op("dve", lambda e, l=l, s=s: e.tensor_scalar(
                    out=Asc[:, l, s, :, :], in0=ada[:, l, (3 * s + 1) * 8:(3 * s + 2) * 8, :],
                    scalar1=1.0, scalar2=None, op0=ALU.add), reads=[("ada",)], writes=[("A",)])
                tr.op("dve", lambda e, l=l, s=s: e.tensor_tensor(
                    out=Asc[:, l, s, :, :], in0=Asc[:, l, s, :, :],
                    in1=par[:, P_NG + (l * 3 + s) * 8:P_NG + (l * 3 + s + 1) * 8].unsqueeze(2).broadcast_to([128, DC, BPC]),
                    op=ALU.mult), reads=[("A",), ("P",)], writes=[("A",)])
                tr.op("dve", lambda e, l=l, s=s: e.tensor_scalar(
                    out=Gsc[:, l, s, :, :], in0=ada[:, l, (3 * s + 2) * 8:(3 * s + 3) * 8, :],
                    scalar1=(1.0 if s == 1 else 0.5), scalar2=None, op0=ALU.mult),
                    reads=[("ada",)], writes=[("G",)])

        CONSTS = [("P",), ("A",), ("G",), ("ada",), ("ones",), ("mh",), ("ident",), ("wmbf",)]

        def group_list(l):
            gl = []
            for gg in range(11):
                gl.append(("gu", 0, gg))
            for mg in range(8):
                gl.append(("dn", 0, mg))
            gl += [("mi", 1, 0), ("mi", 3, 0), ("mi", 4, 0), ("mi", 2, 0), ("mi", 0, 0),
                   ("mi", 1, 1), ("mi", 3, 1), ("mo", 0, 0), ("mo", 1, 0),
                   ("mi", 4, 1), ("mi", 2, 1), ("mi", 0, 1), ("mo", 0, 1), ("mo", 1, 1)]
            for gg in range(11):
                gl.append(("gu", 1, gg))
            for mg in range(8):
                gl.append(("dn", 1, mg))
            return gl

        def gid_of(l, g):
            kind, a, b = g
            base = l * NGL
            if kind == "gu":
                return base + (0 if a == 0 else 26) + b
            if kind == "dn":
                return base + (11 if a == 0 else 37) + b
            if kind == "mi":
                return base + 19 + a
            return base + 24 + a

        def group_len(g):
            return DNL if g[0] == "dn" else 4096

        seq = []
        for p in range(npass):
            for l in range(nl):
                for g in group_list(l):
                    seq.append((p, l, g))
        state = dict(next_load=0, cur=0)
        converted = set()

        def emit_load(n):
            p, l, g = seq[n]
            s = n % NSLOT
            gid = gid_of(l, g)
            L = group_len(g)
            slot = ring[s]
            kind, a, b = g
            if gid not in converted:
                converted.add(gid)
                fns = []
                if kind == "gu":
                    wv = wgu_d[a][l].rearrange("(c p) e -> p c e", p=128)
                    sv = slot[:, 0:4096].rearrange("p (c w) -> p c w", w=512)
                    fns.append(lambda e, wv=wv, sv=sv, b=b: e.dma_start(out=sv[:, :, 0:256], in_=wv[:, :, b * 256:(b + 1) * 256]))
                    fns.append(lambda e, wv=wv, sv=sv, b=b: e.dma_start(out=sv[:, :, 256:512], in_=wv[:, :, FF + b * 256:FF + (b + 1) * 256]))
                elif kind == "dn":
                    wv = wdn_d[a][l].rearrange("(k p) d -> p k d", p=128)
                    sv = slot[:, 0:DNL].rearrange("p (k w) -> p k w", w=128)
                    fns.append(lambda e, wv=wv, sv=sv, b=b: e.dma_start(out=sv, in_=wv[:, :, b * 128:(b + 1) * 128]))
                elif kind == "mi":
                    wv = win_d[l].rearrange("(c p) e -> p c e", p=128)
                    sv = slot[:, 0:4096].rearrange("p (c w) -> p c w", w=512)
                    fns.append(lambda e, wv=wv, sv=sv, a=a: e.dma_start(out=sv, in_=wv[:, :, a * 512:(a + 1) * 512]))
                else:
                    wv = wout_d[l].rearrange("(k p) d -> p k d", p=128)
                    sv = slot[:, 0:4096].rearrange("p (k w) -> p k w", w=512)
                    fns.append(lambda e, wv=wv, sv=sv, a=a: e.dma_start(out=sv, in_=wv[:, :, a * 512:(a + 1) * 512]))
                tr.dma("pool", fns, f"d_ringc{s}", writes=[("r", s)])
                tr.dma("sp", [lambda e, slot=slot, gid=gid, L=L: e.dma_start(out=scr_d[gid][:, 0:L], in_=slot[:, 0:L])],
                       f"d_wb{s}", reads=[("r", s)], writes=[("scr", gid)])
            else:
                tr.dma("sp", [lambda e, slot=slot, gid=gid, L=L: e.dma_start(out=slot[:, 0:L], in_=scr_d[gid][:, 0:L])],
                       f"d_ring{s}", reads=[("scr", gid)], writes=[("r", s)])

        def acquire(expect, hold=0):
            n = state["cur"]
            assert seq[n] == expect, (seq[n], expect)
            while state["next_load"] < len(seq) and state["next_load"] <= n - hold + NSLOT - 1:
                emit_load(state["next_load"])
                state["next_load"] += 1
            state["cur"] += 1
            s = n % NSLOT
            return ring[s], ("r", s)

        def xkeys(cs, t):
            bid = t * 2 + cur["p"] % 2
            return [("x", bid, c) for c in cs]

        def tsl(t):
            return slice(t * TT, (t + 1) * TT)

        def rstd_from_ss(ss_ap, ss_key, inv_n, r_dst=None):
            m_ap, m_key = ms_r.next()
            tr.op("act", lambda e: e.activation(out=m_ap, in_=ss_ap, func=AF.Ln, bias=epsc[:, 0:1], scale=inv_n),
                  reads=[ss_key, ("mh",)], writes=[m_key])
            r_ap, r_key = r_dst if r_dst is not None else rstd_r.next()
            tr.op("act", lambda e: e.activation(out=r_ap, in_=m_ap, func=AF.Exp, scale=-0.5),
                  reads=[m_key], writes=[r_key])
            return r_ap, r_key

        def stats_rstd(src_fn, nchunks, src_keys_fn, inv_n):
            ss_ap, ss_key = ps_r.next()
            for c in range(nchunks):
                q_ap, q_key = sqb_r.next()
                tr.op("act", lambda e, q_ap=q_ap, src=src_fn(c): e.activation(out=q_ap, in_=src, func=AF.Square),
                      reads=src_keys_fn(c), writes=[q_key])
                tr.op("pe", lambda e, q_ap=q_ap, c=c, ss_ap=ss_ap: e.matmul(
                    ss_ap, lhsT=ones_bf[:, :], rhs=q_ap, start=(c == 0), stop=(c == nchunks - 1)),
                    reads=[q_key, ("ones",)], writes=[ss_key])
            return rstd_from_ss(ss_ap, ss_key, inv_n)

        def apply_mod(l, s, t, b, r_ap, r_key, pp=None):
            pp = cur["p"] if pp is None else pp
            for c in range(DC):
                tm_ap, tm_key = tmpn_r.next()
                tr.op("dve", lambda e, c=c, tm_ap=tm_ap, xa=xt[t][pp % 2][:, c, :]: e.scalar_tensor_tensor(
                    out=tm_ap, in0=xa, scalar=Asc[:, l, s, c, b:b + 1], in1=r_ap,
                    op0=ALU.mult, op1=ALU.mult), reads=[("x", t * 2 + pp % 2, c), r_key, ("A",)], writes=[tm_key])
                m = 3 * s * 8 + c
                tr.op("act", lambda e, c=c, tm_ap=tm_ap, m=m: e.activation(
                    out=hT[:, c, tsl(t)], in_=tm_ap, func=AF.Identity, bias=ada[:, l, m, b:b + 1], scale=1.0),
                    reads=[tm_key, ("ada",)], writes=[("h", c, t)])

        def norm_mod(l, s, t, b):
            r_ap, r_key = stats_rstd(lambda c: xs(c, t), DC, lambda c: xkeys([c], t), 1.0 / D)
            apply_mod(l, s, t, b, r_ap, r_key)

        def next_norm(l, s):
            if s < 2:
                return (l, s + 1)
            return (l + 1, 0) if l + 1 < nl else None

        class NormAcc:
            def __init__(self, t, on_done, src=None, defer=False):
                self.t, self.on_done, self.n = t, on_done, 0
                self.src, self.defer = src, defer
                self.ss_ap, self.ss_key = ps_r.next()
                ps_r.skip.add(self.ss_key[1])

            def add(self, c):
                t, n, ss_ap, ss_key = self.t, self.n, self.ss_ap, self.ss_key
                q_ap, q_key = sqb_r.next()
                xa, xk = (xs(c, t), xkeys([c], t)) if self.src is None else self.src(c)
                tr.op("act", lambda e: e.activation(out=q_ap, in_=xa, func=AF.Square),
                      reads=xk, writes=[q_key])
                tr.op("pe", lambda e: e.matmul(ss_ap, lhsT=ones_bf[:, :], rhs=q_ap, start=(n == 0), stop=(n == DC - 1)),
                      reads=[q_key, ("ones",)], writes=[ss_key])
                self.n += 1
                if self.n == DC and not self.defer:
                    self.finish()

            def finish(self):
                r_ap, r_key = rstd_from_ss(self.ss_ap, self.ss_key, 1.0 / D)
                ps_r.skip.discard(self.ss_key[1])
                self.on_done(self.t, r_ap, r_key)

        class Lagged:
            def __init__(self, accs):
                self.accs, self.pending, self.deferred = accs, None, []

            def step(self, m, t):
                for fn in self.deferred:
                    fn()
                self.deferred = []
                if self.pending is not None:
                    self.accs[self.pending[1]].add(self.pending[0])
                self.pending = (m, t)

            def flush(self):
                for fn in self.deferred:
                    fn()
                self.deferred = []
                if self.pending is not None:
                    self.accs[self.pending[1]].add(self.pending[0])
                    self.pending = None
                for fn in self.deferred:
                    fn()
                self.deferred = []

        def ffn(p, l, f, b, pre_normed, hooks=None):
            s = 0 if f == 0 else 2
            tr.fence("a", ("u", "pg"))
            if not pre_normed:
                for t in range(NT):
                    norm_mod(l, s, t, b)
            for gg in range(11):
                if hooks and gg in hooks:
                    hooks[gg]()
                slot, rkey = acquire((p, l, ("gu", f, gg)))
                sv = slot[:, 0:4096].rearrange("p (c w) -> p c w", w=512)
                for t in range(NT):
                    for jj in range(2):
                        j = gg * 2 + jj
                        g_ap, g_key = ps_r.next()
                        u_ap, u_key = ps_r.next()
                        hke = [[("h", c, t)] for c in range(DC)]
                        tr.mmg_fine([(g_ap, sv[:, c, jj * 128:(jj + 1) * 128], hT[:, c, tsl(t)], c == 0, c == DC - 1)
                                     for c in range(DC)], hke, [rkey], [g_key])
                        tr.mmg_fine([(u_ap, sv[:, c, 256 + jj * 128:256 + (jj + 1) * 128], hT[:, c, tsl(t)], c == 0, c == DC - 1)
                                     for c in range(DC)], hke, [rkey], [u_key])
                        sg_ap, sg_key = sg_r.next()
                        tr.op("act", lambda e, sg_ap=sg_ap, g_ap=g_ap: e.activation(out=sg_ap, in_=g_ap, func=AF.Silu),
                              reads=[g_key], writes=[sg_key])
                        tr.op("dve", lambda e, sg_ap=sg_ap, u_ap=u_ap, j=j, t=t: e.tensor_tensor(
                            out=act_v[:, j, tsl(t)], in0=sg_ap, in1=u_ap, op=ALU.mult),
                            reads=[sg_key, u_key], writes=[("a", j, t)])
            nxt = next_norm(l, s)
            lag = Lagged(None)
            if nxt is not None:
                done = lambda t, r_ap, r_key: apply_mod(nxt[0], nxt[1], t, b, r_ap, r_key)
            else:
                done = lambda t, r_ap, r_key: lag.deferred.append(lambda: final_tile(p, t, r_ap, r_key))
            lag.accs = {t: NormAcc(t, done) for t in range(NT)}
            extra = []
            if nxt is None and p + 1 < npass:
                for t in range(NT):
                    nb = t * 2 + (p + 1) % 2
                    acc = NormAcc(t, lambda t_, r_ap, r_key: apply_mod(0, 0, t_, (p + 1) // 2, r_ap, r_key, pp=p + 1),
                                  src=lambda c, t=t, nb=nb: (xt[t][(p + 1) % 2][:, c, :], [("x", nb, c)]))
                    next_acc[t] = acc
                    extra += [(acc, c) for c in range(DC)]

            def extra_step():
                if extra:
                    acc, c = extra.pop(0)
                    acc.add(c)
            def down_step(m, t, sv, rkey):
                o_ap, o_key = ps_r.next()
                tr.mmg_fine([(o_ap, sv[:, k, :], act_v[:, k, tsl(t)], k == 0, k == FC - 1)
                             for k in range(FC)], [[("a", k, t)] for k in range(FC)], [rkey], [o_key])
                tr.op("dve", lambda e, o_ap=o_ap, m=m, xa=xs(m, t): e.scalar_tensor_tensor(
                    out=xa, in0=o_ap, scalar=Gsc[:, l, s, m, b:b + 1], in1=xa,
                    op0=ALU.mult, op1=ALU.add), reads=[o_key, ("G",)] + xkeys([m], t), writes=xkeys([m], t))
                lag.step(m, t)
                extra_step()

            HOLD = 2
            for m in range(8 - HOLD):
                slot, rkey = acquire((p, l, ("dn", f, m)))
                sv = slot[:, 0:DNL].rearrange("p (k w) -> p k w", w=128)
                for t in range(NT):
                    down_step(m, t, sv, rkey)
            last = []
            for i, m in enumerate(range(8 - HOLD, 8)):
                slot, rkey = acquire((p, l, ("dn", f, m)), hold=i)
                last.append((m, slot[:, 0:DNL].rearrange("p (k w) -> p k w", w=128), rkey))
            for t in range(NT):
                for m, sv, rkey in last:
                    down_step(m, t, sv, rkey)
            lag.flush()

        def group_norm(l, grp, ysrc, ykey):
            r_ap, r_key = stats_rstd(lambda c: ysrc[:, c, :], 4, lambda c: [(ykey, c)], 1.0 / 512)
            for fc in range(4):
                cc = grp * 4 + fc
                tr.op("dve", lambda e, fc=fc, cc=cc: e.scalar_tensor_tensor(
                    out=yn[:, cc, :], in0=ysrc[:, fc, :], scalar=pcol(P_ONG + l * 8 + cc), in1=r_ap,
                    op0=ALU.mult, op1=ALU.mult), reads=[(ykey, fc), r_key, ("P",)], writes=[("yn", cc)])

        def mixer_tile(p, l, t, b, half):
            hk = [("h", c, t) for c in range(DC)]
            zk = [("z", fc) for fc in range(4)]

            def proj_group(gi, fc, sv, rkey):
                o_ap, o_key = ps_r.next()
                tr.mmg_fine([(o_ap, sv[:, c, fc * 128:(fc + 1) * 128], hT[:, c, tsl(t)], c == 0, c == DC - 1)
                             for c in range(DC)], [[k_] for k_ in hk], [rkey], [o_key])
                return o_ap, o_key

            slot, rkey = acquire((p, l, ("mi", 1, t)))
            sv = slot[:, 0:4096].rearrange("p (c w) -> p c w", w=512)
            vinfo = []
            for tk in range(4):
                v_ap, v_key = ps_r.next()
                tok = slice(t * TT + tk * 128, t * TT + (tk + 1) * 128)
                tr.mmg_fine([(v_ap, hT[:, c, tok], sv[:, c, :], c == 0, c == DC - 1) for c in range(DC)],
                            [[k_] for k_ in hk], [rkey], [v_key])
                vg, vg_key = vg_r.next()
                tr.op("act", lambda e, vg=vg, v_ap=v_ap: e.activation(out=vg, in_=v_ap, func=AF.Gelu),
                      reads=[v_key], writes=[vg_key])
                vg3 = vg.rearrange("p (h d) -> p h d", d=64)
                s1, s1k = st_r.next()
                s2, s2k = st_r.next()
                mean, mk = st_r.next()
                msq, msk = st_r.next()
                var, vk = st_r.next()
                rs, rk = st_r.next()
                tr.op("dve", lambda e, s1=s1, vg3=vg3: e.tensor_reduce(out=s1, in_=vg3, axis=AX.X, op=ALU.add),
                      reads=[vg_key], writes=[s1k])
                tr.op("act", lambda e, vg=vg: e.activation(out=vsq, in_=vg, func=AF.Square),
                      reads=[vg_key], writes=[("u_vsq",)])
                tr.op("dve", lambda e, s2=s2: e.tensor_reduce(
                    out=s2, in_=vsq.rearrange("p (h d) -> p h d", d=64), axis=AX.X, op=ALU.add),
                    reads=[("u_vsq",)], writes=[s2k])
                tr.op("dve", lambda e, mean=mean, s1=s1: e.tensor_scalar(
                    out=mean, in0=s1, scalar1=1.0 / 64, scalar2=None, op0=ALU.mult), reads=[s1k], writes=[mk])
                tr.op("dve", lambda e, mean=mean, msq=msq: e.tensor_tensor(out=msq, in0=mean, in1=mean, op=ALU.mult),
                      reads=[mk], writes=[msk])
                tr.op("dve", lambda e, var=var, s2=s2, msq=msq: e.scalar_tensor_tensor(
                    out=var, in0=s2, scalar=1.0 / 64, in1=msq, op0=ALU.mult, op1=ALU.subtract),
                    reads=[s2k, msk], writes=[vk])
                tr.op("dve", lambda e, var=var: e.tensor_scalar(
                    out=var, in0=var, scalar1=EPS, scalar2=None, op0=ALU.add), reads=[vk], writes=[vk])
                tr.op("pool", lambda e, rs=rs, var=var: e.tensor_tensor(out=rs, in0=var, in1=mhalf[:, 0:8], op=ALU.pow),
                      reads=[vk, ("mh",)], writes=[rk])
                vinfo.append((vg, vg_key, vg3, mean, mk, rs, rk))
            yield "v1"
            first = (half == 0 and t == 0)
            if first:
                tr.op("pool", lambda e: e.memset(z[:, :, 0:2], 0.0), writes=zk)
            elif t == 0:
                tr.op("dve", lambda e: e.tensor_copy(out=z[:, :, 0:2], in_=halo[:, l, :, :]),
                      reads=[("halo", l)], writes=zk)
            else:
                tr.op("dve", lambda e: e.tensor_copy(out=z[:, :, 0:2], in_=z[:, :, TT:TT + 2]),
                      reads=zk, writes=zk)
            slot, rkey = acquire((p, l, ("mi", 3, t)))
            sv = slot[:, 0:4096].rearrange("p (c w) -> p c w", w=512)
            for fc in range(4):
                c_ap, c_key = proj_group(3, fc, sv, rkey)
                tr.op("act", lambda e, c_ap=c_ap, fc=fc: e.activation(out=z[:, fc, 2:TT + 2], in_=c_ap, func=AF.Copy),
                      reads=[c_key], writes=[("z", fc)])
            for tk in range(4):
                vg, vg_key, vg3, mean, mk, rs, rk = vinfo[tk]
                tr.op("dve", lambda e, vg3=vg3, mean=mean: e.tensor_tensor(
                    out=vg3, in0=vg3, in1=mean.unsqueeze(2).broadcast_to([128, 8, 64]), op=ALU.subtract),
                    reads=[vg_key, mk], writes=[vg_key])
                tr.op("dve", lambda e, vg3=vg3, rs=rs, tk=tk: e.tensor_tensor(
                    out=vhat[:, tk, :].rearrange("p (h d) -> p h d", d=64), in0=vg3,
                    in1=rs.unsqueeze(2).broadcast_to([128, 8, 64]), op=ALU.mult),
                    reads=[vg_key, rk], writes=[("u_vhat", tk)])
            yield "c_v2"
            slot, rkey = acquire((p, l, ("mi", 4, t)))
            sv = slot[:, 0:4096].rearrange("p (c w) -> p c w", w=512)
            for fc in range(4):
                xb_ap, xb_key = proj_group(4, fc, sv, rkey)
                tr.op("dve", lambda e, xb_ap=xb_ap, fc=fc: e.tensor_tensor(
                    out=z[:, fc, 2:TT + 2], in0=z[:, fc, 2:TT + 2], in1=xb_ap, op=ALU.mult),
                    reads=[("z", fc), xb_key], writes=[("z", fc)])
            if t == NT - 1:
                tr.op("dve", lambda e: e.tensor_copy(out=halo[:, l, :, :], in_=z[:, :, TT:TT + 2]),
                      reads=zk, writes=[("halo", l)])
            for fc in range(4):
                cw = lambda k, fc=fc: pcol(P_CW + (l * 3 + k) * 4 + fc)
                tr.op("dve", lambda e, fc=fc, cw=cw: e.tensor_scalar(
                    out=ybv[:, fc, :], in0=z[:, fc, 2:TT + 2], scalar1=cw(2), scalar2=None, op0=ALU.mult),
                    reads=[("z", fc), ("P",)], writes=[("u_yb", fc)])
                tr.op("dve", lambda e, fc=fc, cw=cw: e.scalar_tensor_tensor(
                    out=ybv[:, fc, :], in0=z[:, fc, 1:TT + 1], scalar=cw(1), in1=ybv[:, fc, :], op0=ALU.mult, op1=ALU.add),
                    reads=[("z", fc), ("P",), ("u_yb", fc)], writes=[("u_yb", fc)])
                tr.op("dve", lambda e, fc=fc, cw=cw: e.scalar_tensor_tensor(
                    out=ybv[:, fc, :], in0=z[:, fc, 0:TT], scalar=cw(0), in1=ybv[:, fc, :], op0=ALU.mult, op1=ALU.add),
                    reads=[("z", fc), ("P",), ("u_yb", fc)], writes=[("u_yb", fc)])
            slot, rkey = acquire((p, l, ("mi", 2, t)))
            sv = slot[:, 0:4096].rearrange("p (c w) -> p c w", w=512)
            for fc in range(4):
                b_ap, b_key = proj_group(2, fc, sv, rkey)
                tr.op("dve", lambda e, b_ap=b_ap, fc=fc: e.tensor_tensor(
                    out=ybv[:, fc, :], in0=ybv[:, fc, :], in1=b_ap, op=ALU.mult),
                    reads=[("u_yb", fc), b_key], writes=[("u_yb", fc)])
            yield "convB"
            slot, rkey = acquire((p, l, ("mi", 0, t)))
            sv = slot[:, 0:4096].rearrange("p (c w) -> p c w", w=512)
            for fc in range(4):
                u_ap, u_key = proj_group(0, fc, sv, rkey)
                ug, ug_key = ug_r.next()
                tr.op("act", lambda e, ug=ug, u_ap=u_ap: e.activation(out=ug, in_=u_ap, func=AF.Gelu),
                      reads=[u_key], writes=[ug_key])
                m_ap, m_key = ps_r.next()
                mms = []
                for tk in range(4):
                    for hh in range(2):
                        mms.append((m_ap[hh * 64:(hh + 1) * 64, tk * 128:(tk + 1) * 128],
                                    vhat[:, tk, fc * 128 + hh * 64:fc * 128 + (hh + 1) * 64],
                                    wm_bf[:, l, 2 * fc + hh, :], True, True))
                tr.mmg(mms, reads=[("u_vhat", tk) for tk in range(4)] + [("wmbf",)], writes=[m_key])
                tm, tm_key = tm_r.next()
                tr.op("dve", lambda e, tm=tm, m_ap=m_ap, fc=fc: e.scalar_tensor_tensor(
                    out=tm.rearrange("p (k t) -> p k t", t=128), in0=m_ap.rearrange("p (k t) -> p k t", t=128),
                    scalar=pcol(P_LNGC + l), in1=Ct[:, l, fc, :].unsqueeze(1).broadcast_to([128, 4, 128]),
                    op0=ALU.mult, op1=ALU.add), reads=[m_key, ("P",), ("Ct",)], writes=[tm_key])
                tr.op("dve", lambda e, tm=tm, ug=ug, fc=fc: e.tensor_tensor(
                    out=yav[:, fc, :], in0=tm, in1=ug, op=ALU.mult), reads=[tm_key, ug_key], writes=[("u_ya", fc)])
            yield "uS"
            group_norm(l, 1, ybv, "u_yb")
            group_norm(l, 0, yav, "u_ya")
            yield "normA"
            lag = Lagged({t: NormAcc(t, lambda t_, r_ap, r_key: apply_mod(l, 2, t_, b, r_ap, r_key))})
            for og in range(2):
                slot, rkey = acquire((p, l, ("mo", og, t)))
                sv = slot[:, 0:4096].rearrange("p (k w) -> p k w", w=512)
                for mm in range(4):
                    m = og * 4 + mm
                    o_ap, o_key = ps_r.next()
                    tr.mmg_fine([(o_ap, sv[:, k, mm * 128:(mm + 1) * 128], yn[:, k, :], k == 0, k == DC - 1)
                                 for k in range(DC)], [[("yn", k)] for k in range(DC)], [rkey], [o_key])
                    tr.op("dve", lambda e, o_ap=o_ap, m=m, xa=xs(m, t): e.scalar_tensor_tensor(
                        out=xa, in0=o_ap, scalar=Gsc[:, l, 1, m, b:b + 1], in1=xa,
                        op0=ALU.mult, op1=ALU.add), reads=[o_key, ("G",)] + xkeys([m], t), writes=xkeys([m], t))
                    lag.step(m, t)
            lag.flush()

        next_acc = {}

        def load_tile(p, t):
            buf = xt[t][p % 2]
            bid = t * 2 + p % 2
            tr.dma("sp", [lambda e, buf=buf, t=t, p=p: e.dma_start(out=buf[:, :, :], in_=x_d[p][:, :, t * TT:(t + 1) * TT])],
                   f"d_x{bid}", writes=[("x", bid, c) for c in range(DC)])

        def final_tile(p, t, r_ap, r_key):
            for c in range(DC):
                tr.op("dve", lambda e, c=c, xa=xs(c, t): e.scalar_tensor_tensor(
                    out=xa, in0=xa, scalar=pcol(P_FNG + c), in1=r_ap,
                    op0=ALU.mult, op1=ALU.mult), reads=xkeys([c], t) + [r_key, ("P",)], writes=xkeys([c], t))
            tr.dma("sp", [lambda e, src_all=xt[t][p % 2][:, :, :], t=t, p=p: e.dma_start(
                out=y_d[p][:, :, t * TT:(t + 1) * TT], in_=src_all)],
                f"d_out{t}", reads=xkeys(range(DC), t), writes=[("yout", t)])

        load_tile(0, 0)
        load_tile(0, 1)
        for p in range(npass):
            cur["p"] = p
            b = p // 2
            half = p % 2
            hooks0 = {}
            if p + 1 < npass:
                hooks0 = {3: (lambda p=p: load_tile(p + 1, 0)), 7: (lambda p=p: load_tile(p + 1, 1))}
            for t in range(NT):
                if t in next_acc:
                    assert next_acc.pop(t).n == DC
                else:
                    norm_mod(0, 0, t, b)
            for l in range(nl):
                ffn(p, l, 0, b, pre_normed=True, hooks=(hooks0 if l == 0 else None))
                tr.fence("u", ("a",))
                g0 = mixer_tile(p, l, 0, b, half)
                g1 = mixer_tile(p, l, 1, b, half)

                def run(g, until):
                    for tag in g:
                        if tag == until:
                            return
                run(g0, "uS")
                run(g1, "v1")
                run(g0, "normA")
                run(g1, "c_v2")
                run(g0, None)
                run(g1, None)
                ffn(p, l, 1, b, pre_normed=True)

        out_sems = [(s, v) for s, v in tr.dma_counts.items() if s.startswith("d_out") or s.startswith("d_wb")]

        for E in tr.engs.values():
            for waits, fn, inc in E["ops"]:
                for s, v in waits:
                    tr.sem(s)
                if inc is not None:
                    tr.sem(inc[0])

        _DBG['tr'] = tr
        block = es.enter_context(nc.Block())

        @block.tensor
        def _(e):
            tr.replay("pe", e)

        @block.scalar
        def _(e):
            tr.replay("act", e)

        @block.vector
        def _(e):
            tr.replay("dve", e)

        @block.gpsimd
        def _(e):
            tr.replay("pool", e)

        @block.sync
        def _(e):
            tr.replay("sp", e)
            for s, v in out_sems:
                e.wait_ge(tr.sem(s), v)
    return nc


def _pack_params(inp):
    par = np.zeros((128, NPARAM), np.float32)

    def fm(v):
        v = np.asarray(v, np.float32)
        n = v.shape[-1] // 128
        lead = v.shape[:-1]
        return np.moveaxis(v.reshape(lead + (n, 128)), -1, 0)

    ng = np.stack([inp["norm_ffn1_g"], inp["norm_mix_g"], inp["norm_ffn2_g"]], axis=1)
    par[:, P_NG:P_NG + 48] = fm(ng).reshape(128, 48)
    par[:, P_FNG:P_FNG + 8] = fm(inp["final_norm_g"]).reshape(128, 8)
    par[:, P_ONG:P_ONG + 16] = fm(inp["out_norm_g"]).reshape(128, 16)
    par[:, P_CW:P_CW + 24] = fm(inp["conv_w"]).reshape(128, 24)
    par[:, P_ADAB:P_ADAB + 144] = fm(inp["ada_b"]).reshape(128, 144)
    lng = np.asarray(inp["sgu_ln_g"], np.float32)
    lnb = np.asarray(inp["sgu_ln_b"], np.float32)
    par[:, P_LNGC:P_LNGC + 2] = np.tile(lng, (1, 2)).T
    par[:, P_LNBC:P_LNBC + 2] = np.tile(lnb, (1, 2)).T
    bsv = np.asarray(inp["sgu_b"], np.float32)
    bs = np.repeat(bsv.reshape(2, 4, 2, 1, 128), 64, axis=3).reshape(2, 4, 128, 128)
    par[:, P_BS:P_BS + 1024] = np.transpose(bs, (2, 0, 1, 3)).reshape(128, 1024)
    return par


def make_in_maps(inp, n_cores=N_CORES, npass=8):
    par = _pack_params(inp)
    wm = np.ascontiguousarray(np.transpose(np.asarray(inp["sgu_w_s"], np.float32), (3, 0, 1, 2)))
    shared = {
        "params": par, "wm": wm,
        "ada_w": np.ascontiguousarray(inp["ada_w"], dtype=np.float32),
        "ffn1_w_gu": np.ascontiguousarray(inp["ffn1_w_gu"], dtype=np.float32),
        "ffn2_w_gu": np.ascontiguousarray(inp["ffn2_w_gu"], dtype=np.float32),
        "ffn1_w_down": np.ascontiguousarray(inp["ffn1_w_down"], dtype=np.float32),
        "ffn2_w_down": np.ascontiguousarray(inp["ffn2_w_down"], dtype=np.float32),
        "mix_w_in": np.ascontiguousarray(inp["mix_w_in"], dtype=np.float32),
        "mix_w_out": np.ascontiguousarray(inp["mix_w_out"], dtype=np.float32),
    }
    x = np.asarray(inp["x"], np.float32)
    c = np.asarray(inp["c"], np.float32)
    maps = []
    for i in range(n_cores):
        xi = x[BPC * i:BPC * (i + 1)].reshape(BPC * SEQ, D)[:npass * PT]
        xi = np.ascontiguousarray(xi.reshape(npass, PT, DC, 128).transpose(0, 3, 2, 1))
        ci = c[BPC * i:BPC * (i + 1)]
        cT = np.ascontiguousarray(np.transpose(ci.T.reshape(DC, 128, BPC), (1, 0, 2)))
        m = dict(shared)
        m["x"] = xi
        m["cT"] = cT
        maps.append(m)
    return maps


def unpack_y(y, npass=8):
    return np.asarray(y, np.float32).reshape(npass, 128, DC, PT).transpose(0, 3, 2, 1).reshape(npass * PT, D)


_NC_CACHE = {}


def kernel(**inputs):
    if "full" not in _NC_CACHE:
        _NC_CACHE["full"] = build_program(8, 2)
    nc = _NC_CACHE["full"]
    in_maps = make_in_maps(inputs)
    res = run_bass_kernel_spmd(nc, in_maps, core_ids=list(range(N_CORES)))
    outs = [unpack_y(r["y"]).reshape(BPC, SEQ, D) for r in res.results]
    return np.concatenate(outs, axis=0)
```

```python
import contextlib
import numpy as np
import concourse.bass as bass
import concourse.mybir as mybir
from concourse.bass_utils import run_bass_kernel_spmd

F32 = mybir.dt.float32
BF16 = mybir.dt.bfloat16
AF = mybir.ActivationFunctionType
ALU = mybir.AluOpType
AX = mybir.AxisListType

D = 1024
DC = 8
FF = 2816
FC = 22
TT = 512
NT = 2
PT = 1024
SLOT = 4096
DNL = FC * 128
NSLOT = 4
NGL = 45
EPS = 1e-6
N_CORES = 8
SEQ = 2048
BPC = 4

P_NG = 0
P_FNG = P_NG + 48
P_ONG = P_FNG + 8
P_CW = P_ONG + 16
P_ADAB = P_CW + 24
P_LNGC = P_ADAB + 144
P_LNBC = P_LNGC + 2
P_BS = P_LNBC + 2
NPARAM = P_BS + 1024


class Tracker:
    def __init__(self, nc, es):
        self.nc, self.es = nc, es
        self.engs = {n: dict(ops=[], count=0, waited={}) for n in ("pe", "act", "dve", "pool", "sp")}
        self.lastw = {}
        self.rds = {}
        self.dma_counts = {}
        self.sem_objs = {}
        self.fences = {}

    def sem(self, name):
        if name not in self.sem_objs:
            self.sem_objs[name] = self.es.enter_context(self.nc.semaphore(name))
        return self.sem_objs[name]

    @staticmethod
    def _tag(k):
        return k[0].split("_")[0]

    def fence(self, to_tag, from_tags):
        need = {}
        for k, ev in self.lastw.items():
            if self._tag(k) in from_tags and ev is not None:
                if ev[1] > need.get(ev[0], 0):
                    need[ev[0]] = ev[1]
        for k, d in self.rds.items():
            if self._tag(k) in from_tags:
                for s, v in d.items():
                    if v > need.get(s, 0):
                        need[s] = v
        self.fences[to_tag] = need

    def _waits(self, eng, reads, writes):
        need = {}

        def add(s, v):
            if v > need.get(s, 0):
                need[s] = v

        for k in reads:
            ev = self.lastw.get(k)
            if ev is not None:
                add(*ev)
            f = self.fences.get(self._tag(k))
            if f:
                for s, v in f.items():
                    add(s, v)
        for k in writes:
            ev = self.lastw.get(k)
            if ev is not None:
                add(*ev)
            for s, v in self.rds.get(k, {}).items():
                add(s, v)
            f = self.fences.get(self._tag(k))
            if f:
                for s, v in f.items():
                    add(s, v)
        E = self.engs[eng]
        out = []
        for s, v in need.items():
            if eng == "pe" and s == "pe":
                continue
            if E["waited"].get(s, 0) >= v:
                continue
            E["waited"][s] = v
            out.append((s, v))
        return out

    def _record(self, ev, reads, writes):
        for k in reads:
            d = self.rds.setdefault(k, {})
            if ev[1] > d.get(ev[0], 0):
                d[ev[0]] = ev[1]
        for k in writes:
            self.lastw[k] = ev
            self.rds[k] = {}

    def op(self, eng, fn, reads=(), writes=(), signal=True):
        E = self.engs[eng]
        waits = self._waits(eng, reads, writes)
        if signal:
            E["count"] += 1
            ev = (eng, E["count"])
        else:
            ev = (eng, E["count"] + 1)
        E["ops"].append((waits, fn, (eng, 1) if signal else None))
        self._record(ev, reads, writes)
        return ev

    def dma(self, q, fns, semname, reads=(), writes=()):
        E = self.engs[q]
        waits = self._waits(q, reads, writes)
        c = self.dma_counts.get(semname, 0)
        for i, fn in enumerate(fns):
            c += 16
            E["ops"].append((waits if i == 0 else [], fn, (semname, 16)))
        self.dma_counts[semname] = c
        ev = (semname, c)
        self._record(ev, reads, writes)
        return ev

    def mmg(self, mms, reads, writes):
        def fn(e, mms=mms):
            ins = None
            for (o, l, r, st, sp) in mms:
                ins = e.matmul(o, lhsT=l, rhs=r, start=st, stop=sp)
            return ins
        return self.op("pe", fn, reads, writes)

    def mmg_fine(self, mms, reads_each, reads_common, writes):
        n = len(mms)
        for i, (o, l, r, st, sp) in enumerate(mms):
            fn = lambda e, o=o, l=l, r=r, st=st, sp=sp: e.matmul(o, lhsT=l, rhs=r, start=st, stop=sp)
            self.op("pe", fn, list(reads_common) + list(reads_each[i]), writes, signal=(i == n - 1))

    def replay(self, eng, handle):
        for waits, fn, inc in self.engs[eng]["ops"]:
            for s, v in waits:
                handle.wait_ge(self.sem(s), v)
            ins = fn(handle)
            if inc is not None:
                ins.then_inc(self.sem(inc[0]), inc[1])


_DBG = {}


class Rot:
    def __init__(self, name, aps):
        self.name, self.aps, self.i = name, aps, 0
        self.skip = set()

    def next(self):
        i = self.i
        while i in self.skip:
            i = (i + 1) % len(self.aps)
        self.i = (i + 1) % len(self.aps)
        return self.aps[i], (self.name, i)


def build_program(npass=8, nl=2):
    nc = bass.Bass("TRN2", target_bir_lowering=False)
    dt = nc.dram_tensor
    x_d = dt("x", [npass, 128, DC, PT], F32, kind="ExternalInput").ap()
    y_d = dt("y", [npass, 128, DC, PT], F32, kind="ExternalOutput").ap()
    cT_d = dt("cT", [128, DC, BPC], F32, kind="ExternalInput").ap()
    par_d = dt("params", [128, NPARAM], F32, kind="ExternalInput").ap()
    wm_d = dt("wm", [128, 2, 8, 128], F32, kind="ExternalInput").ap()
    adaw_d = dt("ada_w", [2, D, 9 * D], F32, kind="ExternalInput").ap()
    wgu_d = [dt("ffn1_w_gu", [2, D, 2 * FF], F32, kind="ExternalInput").ap(),
             dt("ffn2_w_gu", [2, D, 2 * FF], F32, kind="ExternalInput").ap()]
    wdn_d = [dt("ffn1_w_down", [2, FF, D], F32, kind="ExternalInput").ap(),
             dt("ffn2_w_down", [2, FF, D], F32, kind="ExternalInput").ap()]
    win_d = dt("mix_w_in", [2, D, 2560], F32, kind="ExternalInput").ap()
    wout_d = dt("mix_w_out", [2, D, D], F32, kind="ExternalInput").ap()
    scr_d = dt("scr", [2 * NGL, 128, SLOT], BF16, kind="Internal").ap()

    with contextlib.ExitStack() as es:
        sb = lambda name, shape, dtp: es.enter_context(nc.sbuf_tensor(name, shape, dtp))
        xt0 = [sb("x_t0a", [128, DC, TT], F32), sb("x_t0b", [128, DC, TT], F32)]
        xt1 = [sb("x_t1a", [128, DC, TT], F32), sb("x_t1b", [128, DC, TT], F32)]
        xt = [xt0, xt1]
        cur = {"p": 0}

        def xs(c, t):
            return xt[t][cur["p"] % 2][:, c, :]
        hT = sb("hT", [128, DC, PT], BF16)
        U = sb("U", [128, FC * PT], BF16)
        ring = [sb(f"ring{i}", [128, SLOT], BF16) for i in range(NSLOT)]
        yn = sb("yn", [128, DC, TT], BF16)
        z = sb("z", [128, 4, TT + 2], F32)
        sqb_t = sb("sqb", [128, 2, TT], BF16)
        ms_t = sb("ms", [128, 2, TT], F32)
        rstd_t = sb("rstd", [128, 2, TT], F32)
        tmpn_t = sb("tmpn", [128, 2, TT], F32)
        sg_t = sb("sg", [128, 2, TT], F32)
        par = sb("par", [128, NPARAM], F32)
        cT = sb("cTs", [128, DC, BPC], F32)
        cact = sb("cact", [128, DC, BPC], F32)
        ada = sb("ada", [128, 2, 72, BPC], F32)
        Asc = sb("Asc", [128, 2, 3, DC, BPC], F32)
        Gsc = sb("Gsc", [128, 2, 3, DC, BPC], F32)
        ones_bf = sb("ones_bf", [128, 128], BF16)
        mhalf = sb("mhalf", [128, 8], F32)
        epsc = sb("epsc", [128, 8], F32)
        wm_bf = sb("wm_bf", [128, 2, 8, 128], BF16)
        halo = sb("halo", [128, 2, 4, 2], F32)
        Ct = sb("Ct", [128, 2, 4, 128], F32)
        ps = [es.enter_context(nc.psum_tensor(f"ps{i}", [128, TT], F32)) for i in range(8)]

        tr = Tracker(nc, es)

        def uf32(off, n):
            return U[:, off:off + 2 * n].bitcast(F32)

        act_v = U[:, :].rearrange("p (j t) -> p j t", t=PT)
        vg_r = Rot("u_vg", [uf32(i * 1024, 512) for i in range(4)])
        vsq = uf32(4096, 512)
        vhat = U[:, 5120:7168].rearrange("p (k f) -> p k f", f=512)
        ug_r = Rot("u_ug", [uf32(7168, 512), uf32(8192, 512)])
        tm_r = Rot("u_tm", [uf32(9216, 512), uf32(10240, 512)])
        yav = uf32(11264, 2048).rearrange("p (c t) -> p c t", t=512)
        ybv = uf32(15360, 2048).rearrange("p (c t) -> p c t", t=512)
        st_r = Rot("u_st", [uf32(19456 + i * 16, 8) for i in range(48)])
        xn_v = uf32(0, 4096).rearrange("p (c t) -> p c t", t=512)
        adast = [uf32(0, 4096).rearrange("p (c e) -> p c e", e=512),
                 uf32(8192, 4096).rearrange("p (c e) -> p c e", e=512)]
        wm_st = uf32(16384, 2048).rearrange("p (l h t) -> p l h t", l=2, h=8)

        sqb_r = Rot("sqb", [sqb_t[:, i, :] for i in range(2)])
        ms_r = Rot("ms", [ms_t[:, i, :] for i in range(2)])
        rstd_r = Rot("rstd", [rstd_t[:, i, :] for i in range(2)])
        tmpn_r = Rot("tmpn", [tmpn_t[:, i, :] for i in range(2)])
        sg_r = Rot("sg", [sg_t[:, i, :] for i in range(2)])
        ps_r = Rot("ps", [p[:, :] for p in ps])

        def pcol(off):
            return par[:, off:off + 1]

        tr.dma("sp", [lambda e: e.dma_start(out=par[:, :], in_=par_d)], "d_par", writes=[("P",)])
        tr.dma("sp", [lambda e: e.dma_start(out=cT[:, :, :], in_=cT_d)], "d_ct", writes=[("cT",)])
        tr.dma("sp", [lambda e: e.dma_start(out=wm_st, in_=wm_d)], "d_wm", writes=[("pg", "wm")])

        tr.op("pool", lambda e: e.memset(ones_bf[:, :], 1.0), writes=[("ones",)])
        tr.op("pool", lambda e: e.memset(epsc[:, :], EPS), writes=[("mh",)])
        tr.op("pool", lambda e: e.memset(mhalf[:, :], -0.5), writes=[("mh",)])
        for l in range(2):
            for h in range(8):
                tr.op("pool", lambda e, l=l, h=h: e.affine_select(
                    out=wm_st[:, l, h, :], in_=wm_st[:, l, h, :], pattern=[[1, 128]],
                    compare_op=ALU.is_ge, fill=0.0, base=0, channel_multiplier=-1),
                    reads=[("pg", "wm")], writes=[("pg", "wm")])
        tr.op("pool", lambda e: e.tensor_copy(out=wm_bf[:, :, :, :], in_=wm_st),
              reads=[("pg", "wm")], writes=[("wmbf",)])
        tr.op("pool", lambda e: e.memset(halo[:, :, :, :], 0.0), writes=[("halo", 0), ("halo", 1)])

        for l in range(2):
            cb_ap, cb_key = ps_r.next()
            mms = []
            for fc in range(4):
                for hh in range(2):
                    mms.append((cb_ap[hh * 64:(hh + 1) * 64, fc * 128:(fc + 1) * 128], ones_bf[:, 0:64],
                                wm_bf[:, l, 2 * fc + hh, :], True, True))
            tr.mmg(mms, reads=[("wmbf",), ("ones",)], writes=[cb_key])
            tr.op("dve", lambda e, l=l, cb_ap=cb_ap: e.scalar_tensor_tensor(
                out=Ct[:, l, :, :].rearrange("p f t -> p (f t)"), in0=cb_ap, scalar=pcol(P_LNBC + l),
                in1=par[:, P_BS + l * 512:P_BS + (l + 1) * 512], op0=ALU.mult, op1=ALU.add),
                reads=[cb_key, ("P",)], writes=[("Ct",)])

        tr.op("act", lambda e: e.activation(out=cact[:, :, :], in_=cT[:, :, :], func=AF.Silu),
              reads=[("cT",)], writes=[("cact",)])

        for l in range(nl):
            aps_ap, aps_key = ps_r.next()
            ps_r.skip.add(aps_key[1])
            wv = adaw_d[l].rearrange("(c p) e -> p c e", p=128)
            for mg in range(18):
                st_i = mg % 2
                tr.dma("sp", [lambda e, st_i=st_i, mg=mg, wv=wv: e.dma_start(
                    out=adast[st_i], in_=wv[:, :, mg * 512:(mg + 1) * 512])],
                    f"d_ada{st_i}", writes=[("pg", "ada", st_i)])
                mms = []
                for mm in range(4):
                    m = mg * 4 + mm
                    for c in range(DC):
                        mms.append((aps_ap[:, m * 4:(m + 1) * 4], adast[st_i][:, c, mm * 128:(mm + 1) * 128],
                                    cact[:, c, :], c == 0, c == DC - 1))
                tr.mmg(mms, reads=[("pg", "ada", st_i), ("cact",)], writes=[aps_key])
            ps_r.skip.discard(aps_key[1])
            tr.op("dve", lambda e, l=l, aps_ap=aps_ap: e.tensor_tensor(
                out=ada[:, l, :, :], in0=aps_ap[:, 0:288].rearrange("p (m b) -> p m b", b=BPC),
                in1=par[:, P_ADAB + l * 72:P_ADAB + (l + 1) * 72].unsqueeze(2).broadcast_to([128, 72, BPC]),
                op=ALU.add), reads=[aps_key, ("P",)], writes=[("ada",)])
            for s in range(3):
                tr.op("dve", lambda e, l=l, s=s: e.tensor_scalar(
                    out=Asc[:, l, s, :, :], in0=ada[:, l, (3 * s + 1) * 8:(3 * s + 2) * 8, :],
                    scalar1=1.0, scalar2=None, op0=ALU.add), reads=[("ada",)], writes=[("A",)])
                tr.op("dve", lambda e, l=l, s=s: e.tensor_tensor(
                    out=Asc[:, l, s, :, :], in0=Asc[:, l, s, :, :],
                    in1=par[:, P_NG + (l * 3 + s) * 8:P_NG + (l * 3 + s + 1) * 8].unsqueeze(2).broadcast_to([128, DC, BPC]),
                    op=ALU.mult), reads=[("A",), ("P",)], writes=[("A",)])
                tr.op("dve", lambda e, l=l, s=s: e.tensor_scalar(
                    out=Gsc[:, l, s, :, :], in0=ada[:, l, (3 * s + 2) * 8:(3 * s + 3) * 8, :],
                    scalar1=(1.0 if s == 1 else 0.5), scalar2=None, op0=ALU.mult),
                    reads=[("ada",)], writes=[("G",)])

        CONSTS = [("P",), ("A",), ("G",), ("ada",), ("ones",), ("mh",), ("ident",), ("wmbf",)]

        def group_list(l):
            gl = []
            for gg in range(11):
                gl.append(("gu", 0, gg))
            for mg in range(8):
                gl.append(("dn", 0, mg))
            gl += [("mi", 1, 0), ("mi", 3, 0), ("mi", 4, 0), ("mi", 2, 0), ("mi", 0, 0),
                   ("mi", 1, 1), ("mi", 3, 1), ("mo", 0, 0), ("mo", 1, 0),
                   ("mi", 4, 1), ("mi", 2, 1), ("mi", 0, 1), ("mo", 0, 1), ("mo", 1, 1)]
            for gg in range(11):
                gl.append(("gu", 1, gg))
            for mg in range(8):
                gl.append(("dn", 1, mg))
            return gl

        def gid_of(l, g):
            kind, a, b = g
            base = l * NGL
            if kind == "gu":
                return base + (0 if a == 0 else 26) + b
            if kind == "dn":
                return base + (11 if a == 0 else 37) + b
            if kind == "mi":
                return base + 19 + a
            return base + 24 + a

        def group_len(g):
            return DNL if g[0] == "dn" else 4096

        seq = []
        for p in range(npass):
            for l in range(nl):
                for g in group_list(l):
                    seq.append((p, l, g))
        state = dict(next_load=0, cur=0)
        converted = set()

        def emit_load(n):
            p, l, g = seq[n]
            s = n % NSLOT
            gid = gid_of(l, g)
            L = group_len(g)
            slot = ring[s]
            kind, a, b = g
            if gid not in converted:
                converted.add(gid)
                fns = []
                if kind == "gu":
                    wv = wgu_d[a][l].rearrange("(c p) e -> p c e", p=128)
                    sv = slot[:, 0:4096].rearrange("p (c w) -> p c w", w=512)
                    fns.append(lambda e, wv=wv, sv=sv, b=b: e.dma_start(out=sv[:, :, 0:256], in_=wv[:, :, b * 256:(b + 1) * 256]))
                    fns.append(lambda e, wv=wv, sv=sv, b=b: e.dma_start(out=sv[:, :, 256:512], in_=wv[:, :, FF + b * 256:FF + (b + 1) * 256]))
                elif kind == "dn":
                    wv = wdn_d[a][l].rearrange("(k p) d -> p k d", p=128)
                    sv = slot[:, 0:DNL].rearrange("p (k w) -> p k w", w=128)
                    fns.append(lambda e, wv=wv, sv=sv, b=b: e.dma_start(out=sv, in_=wv[:, :, b * 128:(b + 1) * 128]))
                elif kind == "mi":
                    wv = win_d[l].rearrange("(c p) e -> p c e", p=128)
                    sv = slot[:, 0:4096].rearrange("p (c w) -> p c w", w=512)
                    fns.append(lambda e, wv=wv, sv=sv, a=a: e.dma_start(out=sv, in_=wv[:, :, a * 512:(a + 1) * 512]))
                else:
                    wv = wout_d[l].rearrange("(k p) d -> p k d", p=128)
                    sv = slot[:, 0:4096].rearrange("p (k w) -> p k w", w=512)
                    fns.append(lambda e, wv=wv, sv=sv, a=a: e.dma_start(out=sv, in_=wv[:, :, a * 512:(a + 1) * 512]))
                tr.dma("pool", fns, f"d_ringc{s}", writes=[("r", s)])
                tr.dma("sp", [lambda e, slot=slot, gid=gid, L=L: e.dma_start(out=scr_d[gid][:, 0:L], in_=slot[:, 0:L])],
                       f"d_wb{s}", reads=[("r", s)], writes=[("scr", gid)])
            else:
                tr.dma("sp", [lambda e, slot=slot, gid=gid, L=L: e.dma_start(out=slot[:, 0:L], in_=scr_d[gid][:, 0:L])],
                       f"d_ring{s}", reads=[("scr", gid)], writes=[("r", s)])

        def acquire(expect, hold=0):
            n = state["cur"]
            assert seq[n] == expect, (seq[n], expect)
            while state["next_load"] < len(seq) and state["next_load"] <= n - hold + NSLOT - 1:
                emit_load(state["next_load"])
                state["next_load"] += 1
            state["cur"] += 1
            s = n % NSLOT
            return ring[s], ("r", s)

        def xkeys(cs, t):
            bid = t * 2 + cur["p"] % 2
            return [("x", bid, c) for c in cs]

        def tsl(t):
            return slice(t * TT, (t + 1) * TT)

        def rstd_from_ss(ss_ap, ss_key, inv_n, r_dst=None):
            m_ap, m_key = ms_r.next()
            tr.op("act", lambda e: e.activation(out=m_ap, in_=ss_ap, func=AF.Ln, bias=epsc[:, 0:1], scale=inv_n),
                  reads=[ss_key, ("mh",)], writes=[m_key])
            r_ap, r_key = r_dst if r_dst is not None else rstd_r.next()
            tr.op("act", lambda e: e.activation(out=r_ap, in_=m_ap, func=AF.Exp, scale=-0.5),
                  reads=[m_key], writes=[r_key])
            return r_ap, r_key

        def stats_rstd(src_fn, nchunks, src_keys_fn, inv_n):
            ss_ap, ss_key = ps_r.next()
            for c in range(nchunks):
                q_ap, q_key = sqb_r.next()
                tr.op("act", lambda e, q_ap=q_ap, src=src_fn(c): e.activation(out=q_ap, in_=src, func=AF.Square),
                      reads=src_keys_fn(c), writes=[q_key])
                tr.op("pe", lambda e, q_ap=q_ap, c=c, ss_ap=ss_ap: e.matmul(
                    ss_ap, lhsT=ones_bf[:, :], rhs=q_ap, start=(c == 0), stop=(c == nchunks - 1)),
                    reads=[q_key, ("ones",)], writes=[ss_key])
            return rstd_from_ss(ss_ap, ss_key, inv_n)

        def apply_mod(l, s, t, b, r_ap, r_key, pp=None):
            pp = cur["p"] if pp is None else pp
            for c in range(DC):
                tm_ap, tm_key = tmpn_r.next()
                tr.op("dve", lambda e, c=c, tm_ap=tm_ap, xa=xt[t][pp % 2][:, c, :]: e.scalar_tensor_tensor(
                    out=tm_ap, in0=xa, scalar=Asc[:, l, s, c, b:b + 1], in1=r_ap,
                    op0=ALU.mult, op1=ALU.mult), reads=[("x", t * 2 + pp % 2, c), r_key, ("A",)], writes=[tm_key])
                m = 3 * s * 8 + c
                tr.op("act", lambda e, c=c, tm_ap=tm_ap, m=m: e.activation(
                    out=hT[:, c, tsl(t)], in_=tm_ap, func=AF.Identity, bias=ada[:, l, m, b:b + 1], scale=1.0),
                    reads=[tm_key, ("ada",)], writes=[("h", c, t)])

        def norm_mod(l, s, t, b):
            r_ap, r_key = stats_rstd(lambda c: xs(c, t), DC, lambda c: xkeys([c], t), 1.0 / D)
            apply_mod(l, s, t, b, r_ap, r_key)

        def next_norm(l, s):
            if s < 2:
                return (l, s + 1)
            return (l + 1, 0) if l + 1 < nl else None

        class NormAcc:
            def __init__(self, t, on_done, src=None, defer=False):
                self.t, self.on_done, self.n = t, on_done, 0
                self.src, self.defer = src, defer
                self.ss_ap, self.ss_key = ps_r.next()
                ps_r.skip.add(self.ss_key[1])

            def add(self, c):
                t, n, ss_ap, ss_key = self.t, self.n, self.ss_ap, self.ss_key
                q_ap, q_key = sqb_r.next()
                xa, xk = (xs(c, t), xkeys([c], t)) if self.src is None else self.src(c)
                tr.op("act", lambda e: e.activation(out=q_ap, in_=xa, func=AF.Square),
                      reads=xk, writes=[q_key])
                tr.op("pe", lambda e: e.matmul(ss_ap, lhsT=ones_bf[:, :], rhs=q_ap, start=(n == 0), stop=(n == DC - 1)),
                      reads=[q_key, ("ones",)], writes=[ss_key])
                self.n += 1
                if self.n == DC and not self.defer:
                    self.finish()

            def finish(self):
                r_ap, r_key = rstd_from_ss(self.ss_ap, self.ss_key, 1.0 / D)
                ps_r.skip.discard(self.ss_key[1])
                self.on_done(self.t, r_ap, r_key)

        class Lagged:
            def __init__(self, accs):
                self.accs, self.pending, self.deferred = accs, None, []

            def step(self, m, t):
                for fn in self.deferred:
                    fn()
                self.deferred = []
                if self.pending is not None:
                    self.accs[self.pending[1]].add(self.pending[0])
                self.pending = (m, t)

            def flush(self):
                for fn in self.deferred:
                    fn()
                self.deferred = []
                if self.pending is not None:
                    self.accs[self.pending[1]].add(self.pending[0])
                    self.pending = None
                for fn in self.deferred:
                    fn()
                self.deferred = []

        def ffn(p, l, f, b, pre_normed, hooks=None):
            s = 0 if f == 0 else 2
            tr.fence("a", ("u", "pg"))
            if not pre_normed:
                for t in range(NT):
                    norm_mod(l, s, t, b)
            for gg in range(11):
                if hooks and gg in hooks:
                    hooks[gg]()
                slot, rkey = acquire((p, l, ("gu", f, gg)))
                sv = slot[:, 0:4096].rearrange("p (c w) -> p c w", w=512)
                for t in range(NT):
                    for jj in range(2):
                        j = gg * 2 + jj
                        g_ap, g_key = ps_r.next()
                        u_ap, u_key = ps_r.next()
                        hke = [[("h", c, t)] for c in range(DC)]
                        tr.mmg_fine([(g_ap, sv[:, c, jj * 128:(jj + 1) * 128], hT[:, c, tsl(t)], c == 0, c == DC - 1)
                                     for c in range(DC)], hke, [rkey], [g_key])
                        tr.mmg_fine([(u_ap, sv[:, c, 256 + jj * 128:256 + (jj + 1) * 128], hT[:, c, tsl(t)], c == 0, c == DC - 1)
                                     for c in range(DC)], hke, [rkey], [u_key])
                        sg_ap, sg_key = sg_r.next()
                        tr.op("act", lambda e, sg_ap=sg_ap, g_ap=g_ap: e.activation(out=sg_ap, in_=g_ap, func=AF.Silu),
                              reads=[g_key], writes=[sg_key])
                        tr.op("dve", lambda e, sg_ap=sg_ap, u_ap=u_ap, j=j, t=t: e.tensor_tensor(
                            out=act_v[:, j, tsl(t)], in0=sg_ap, in1=u_ap, op=ALU.mult),
                            reads=[sg_key, u_key], writes=[("a", j, t)])
            nxt = next_norm(l, s)
            lag = Lagged(None)
            if nxt is not None:
                done = lambda t, r_ap, r_key: apply_mod(nxt[0], nxt[1], t, b, r_ap, r_key)
            else:
                done = lambda t, r_ap, r_key: lag.deferred.append(lambda: final_tile(p, t, r_ap, r_key))
            lag.accs = {t: NormAcc(t, done) for t in range(NT)}
            extra = []
            if nxt is None and p + 1 < npass:
                for t in range(NT):
                    nb = t * 2 + (p + 1) % 2
                    acc = NormAcc(t, lambda t_, r_ap, r_key: apply_mod(0, 0, t_, (p + 1) // 2, r_ap, r_key, pp=p + 1),
                                  src=lambda c, t=t, nb=nb: (xt[t][(p + 1) % 2][:, c, :], [("x", nb, c)]))
                    next_acc[t] = acc
                    extra += [(acc, c) for c in range(DC)]

            def extra_step():
                for _ in range(2):
                    if extra:
                        acc, c = extra.pop(0)
                        acc.add(c)
            def down_step(m, t, sv, rkey):
                o_ap, o_key = ps_r.next()
                tr.mmg_fine([(o_ap, sv[:, k, :], act_v[:, k, tsl(t)], k == 0, k == FC - 1)
                             for k in range(FC)], [[("a", k, t)] for k in range(FC)], [rkey], [o_key])
                tr.op("dve", lambda e, o_ap=o_ap, m=m, xa=xs(m, t): e.scalar_tensor_tensor(
                    out=xa, in0=o_ap, scalar=Gsc[:, l, s, m, b:b + 1], in1=xa,
                    op0=ALU.mult, op1=ALU.add), reads=[o_key, ("G",)] + xkeys([m], t), writes=xkeys([m], t))
                lag.step(m, t)
                extra_step()

            HOLD = 2
            for m in range(8 - HOLD):
                slot, rkey = acquire((p, l, ("dn", f, m)))
                sv = slot[:, 0:DNL].rearrange("p (k w) -> p k w", w=128)
                for t in range(NT):
                    down_step(m, t, sv, rkey)
            last = []
            for i, m in enumerate(range(8 - HOLD, 8)):
                slot, rkey = acquire((p, l, ("dn", f, m)), hold=i)
                last.append((m, slot[:, 0:DNL].rearrange("p (k w) -> p k w", w=128), rkey))
            for t in range(NT):
                for m, sv, rkey in last:
                    down_step(m, t, sv, rkey)
            lag.flush()

        def group_norm(l, grp, ysrc, ykey):
            r_ap, r_key = stats_rstd(lambda c: ysrc[:, c, :], 4, lambda c: [(ykey, c)], 1.0 / 512)
            for fc in range(4):
                cc = grp * 4 + fc
                tr.op("dve", lambda e, fc=fc, cc=cc: e.scalar_tensor_tensor(
                    out=yn[:, cc, :], in0=ysrc[:, fc, :], scalar=pcol(P_ONG + l * 8 + cc), in1=r_ap,
                    op0=ALU.mult, op1=ALU.mult), reads=[(ykey, fc), r_key, ("P",)], writes=[("yn", cc)])

        def mixer_tile(p, l, t, b, half):
            hk = [("h", c, t) for c in range(DC)]
            zk = [("z", fc) for fc in range(4)]

            def proj_group(gi, fc, sv, rkey):
                o_ap, o_key = ps_r.next()
                tr.mmg_fine([(o_ap, sv[:, c, fc * 128:(fc + 1) * 128], hT[:, c, tsl(t)], c == 0, c == DC - 1)
                             for c in range(DC)], [[k_] for k_ in hk], [rkey], [o_key])
                return o_ap, o_key

            slot, rkey = acquire((p, l, ("mi", 1, t)))
            sv = slot[:, 0:4096].rearrange("p (c w) -> p c w", w=512)
            vinfo = []
            for tk in range(4):
                v_ap, v_key = ps_r.next()
                tok = slice(t * TT + tk * 128, t * TT + (tk + 1) * 128)
                tr.mmg_fine([(v_ap, hT[:, c, tok], sv[:, c, :], c == 0, c == DC - 1) for c in range(DC)],
                            [[k_] for k_ in hk], [rkey], [v_key])
                vg, vg_key = vg_r.next()
                tr.op("act", lambda e, vg=vg, v_ap=v_ap: e.activation(out=vg, in_=v_ap, func=AF.Gelu),
                      reads=[v_key], writes=[vg_key])
                vg3 = vg.rearrange("p (h d) -> p h d", d=64)
                s1, s1k = st_r.next()
                s2, s2k = st_r.next()
                mean, mk = st_r.next()
                msq, msk = st_r.next()
                var, vk = st_r.next()
                rs, rk = st_r.next()
                tr.op("dve", lambda e, s1=s1, vg3=vg3: e.tensor_reduce(out=s1, in_=vg3, axis=AX.X, op=ALU.add),
                      reads=[vg_key], writes=[s1k])
                tr.op("act", lambda e, vg=vg: e.activation(out=vsq, in_=vg, func=AF.Square),
                      reads=[vg_key], writes=[("u_vsq",)])
                tr.op("dve", lambda e, s2=s2: e.tensor_reduce(
                    out=s2, in_=vsq.rearrange("p (h d) -> p h d", d=64), axis=AX.X, op=ALU.add),
                    reads=[("u_vsq",)], writes=[s2k])
                tr.op("dve", lambda e, mean=mean, s1=s1: e.tensor_scalar(
                    out=mean, in0=s1, scalar1=1.0 / 64, scalar2=None, op0=ALU.mult), reads=[s1k], writes=[mk])
                tr.op("dve", lambda e, mean=mean, msq=msq: e.tensor_tensor(out=msq, in0=mean, in1=mean, op=ALU.mult),
                      reads=[mk], writes=[msk])
                tr.op("dve", lambda e, var=var, s2=s2, msq=msq: e.scalar_tensor_tensor(
                    out=var, in0=s2, scalar=1.0 / 64, in1=msq, op0=ALU.mult, op1=ALU.subtract),
                    reads=[s2k, msk], writes=[vk])
                tr.op("dve", lambda e, var=var: e.tensor_scalar(
                    out=var, in0=var, scalar1=EPS, scalar2=None, op0=ALU.add), reads=[vk], writes=[vk])
                tr.op("pool", lambda e, rs=rs, var=var: e.tensor_tensor(out=rs, in0=var, in1=mhalf[:, 0:8], op=ALU.pow),
                      reads=[vk, ("mh",)], writes=[rk])
                vinfo.append((vg, vg_key, vg3, mean, mk, rs, rk))
            yield "v1"
            first = (half == 0 and t == 0)
            if first:
                tr.op("pool", lambda e: e.memset(z[:, :, 0:2], 0.0), writes=zk)
            elif t == 0:
                tr.op("dve", lambda e: e.tensor_copy(out=z[:, :, 0:2], in_=halo[:, l, :, :]),
                      reads=[("halo", l)], writes=zk)
            else:
                tr.op("dve", lambda e: e.tensor_copy(out=z[:, :, 0:2], in_=z[:, :, TT:TT + 2]),
                      reads=zk, writes=zk)
            slot, rkey = acquire((p, l, ("mi", 3, t)))
            sv = slot[:, 0:4096].rearrange("p (c w) -> p c w", w=512)
            for fc in range(4):
                c_ap, c_key = proj_group(3, fc, sv, rkey)
                tr.op("act", lambda e, c_ap=c_ap, fc=fc: e.activation(out=z[:, fc, 2:TT + 2], in_=c_ap, func=AF.Copy),
                      reads=[c_key], writes=[("z", fc)])
            for tk in range(4):
                vg, vg_key, vg3, mean, mk, rs, rk = vinfo[tk]
                tr.op("dve", lambda e, vg3=vg3, mean=mean: e.tensor_tensor(
                    out=vg3, in0=vg3, in1=mean.unsqueeze(2).broadcast_to([128, 8, 64]), op=ALU.subtract),
                    reads=[vg_key, mk], writes=[vg_key])
                tr.op("dve", lambda e, vg3=vg3, rs=rs, tk=tk: e.tensor_tensor(
                    out=vhat[:, tk, :].rearrange("p (h d) -> p h d", d=64), in0=vg3,
                    in1=rs.unsqueeze(2).broadcast_to([128, 8, 64]), op=ALU.mult),
                    reads=[vg_key, rk], writes=[("u_vhat", tk)])
            yield "c_v2"
            slot, rkey = acquire((p, l, ("mi", 4, t)))
            sv = slot[:, 0:4096].rearrange("p (c w) -> p c w", w=512)
            for fc in range(4):
                xb_ap, xb_key = proj_group(4, fc, sv, rkey)
                tr.op("dve", lambda e, xb_ap=xb_ap, fc=fc: e.tensor_tensor(
                    out=z[:, fc, 2:TT + 2], in0=z[:, fc, 2:TT + 2], in1=xb_ap, op=ALU.mult),
                    reads=[("z", fc), xb_key], writes=[("z", fc)])
            if t == NT - 1:
                tr.op("dve", lambda e: e.tensor_copy(out=halo[:, l, :, :], in_=z[:, :, TT:TT + 2]),
                      reads=zk, writes=[("halo", l)])
            for fc in range(4):
                cw = lambda k, fc=fc: pcol(P_CW + (l * 3 + k) * 4 + fc)
                tr.op("dve", lambda e, fc=fc, cw=cw: e.tensor_scalar(
                    out=ybv[:, fc, :], in0=z[:, fc, 2:TT + 2], scalar1=cw(2), scalar2=None, op0=ALU.mult),
                    reads=[("z", fc), ("P",)], writes=[("u_yb", fc)])
                tr.op("dve", lambda e, fc=fc, cw=cw: e.scalar_tensor_tensor(
                    out=ybv[:, fc, :], in0=z[:, fc, 1:TT + 1], scalar=cw(1), in1=ybv[:, fc, :], op0=ALU.mult, op1=ALU.add),
                    reads=[("z", fc), ("P",), ("u_yb", fc)], writes=[("u_yb", fc)])
                tr.op("dve", lambda e, fc=fc, cw=cw: e.scalar_tensor_tensor(
                    out=ybv[:, fc, :], in0=z[:, fc, 0:TT], scalar=cw(0), in1=ybv[:, fc, :], op0=ALU.mult, op1=ALU.add),
                    reads=[("z", fc), ("P",), ("u_yb", fc)], writes=[("u_yb", fc)])
            slot, rkey = acquire((p, l, ("mi", 2, t)))
            sv = slot[:, 0:4096].rearrange("p (c w) -> p c w", w=512)
            for fc in range(4):
                b_ap, b_key = proj_group(2, fc, sv, rkey)
                tr.op("dve", lambda e, b_ap=b_ap, fc=fc: e.tensor_tensor(
                    out=ybv[:, fc, :], in0=ybv[:, fc, :], in1=b_ap, op=ALU.mult),
                    reads=[("u_yb", fc), b_key], writes=[("u_yb", fc)])
            yield "convB"
            slot, rkey = acquire((p, l, ("mi", 0, t)))
            sv = slot[:, 0:4096].rearrange("p (c w) -> p c w", w=512)
            for fc in range(4):
                u_ap, u_key = proj_group(0, fc, sv, rkey)
                ug, ug_key = ug_r.next()
                tr.op("act", lambda e, ug=ug, u_ap=u_ap: e.activation(out=ug, in_=u_ap, func=AF.Gelu),
                      reads=[u_key], writes=[ug_key])
                m_ap, m_key = ps_r.next()
                mms = []
                for tk in range(4):
                    for hh in range(2):
                        mms.append((m_ap[hh * 64:(hh + 1) * 64, tk * 128:(tk + 1) * 128],
                                    vhat[:, tk, fc * 128 + hh * 64:fc * 128 + (hh + 1) * 64],
                                    wm_bf[:, l, 2 * fc + hh, :], True, True))
                tr.mmg(mms, reads=[("u_vhat", tk) for tk in range(4)] + [("wmbf",)], writes=[m_key])
                tm, tm_key = tm_r.next()
                tr.op("dve", lambda e, tm=tm, m_ap=m_ap, fc=fc: e.scalar_tensor_tensor(
                    out=tm.rearrange("p (k t) -> p k t", t=128), in0=m_ap.rearrange("p (k t) -> p k t", t=128),
                    scalar=pcol(P_LNGC + l), in1=Ct[:, l, fc, :].unsqueeze(1).broadcast_to([128, 4, 128]),
                    op0=ALU.mult, op1=ALU.add), reads=[m_key, ("P",), ("Ct",)], writes=[tm_key])
                tr.op("dve", lambda e, tm=tm, ug=ug, fc=fc: e.tensor_tensor(
                    out=yav[:, fc, :], in0=tm, in1=ug, op=ALU.mult), reads=[tm_key, ug_key], writes=[("u_ya", fc)])
            yield "uS"
            group_norm(l, 1, ybv, "u_yb")
            group_norm(l, 0, yav, "u_ya")
            yield "normA"
            lag = Lagged({t: NormAcc(t, lambda t_, r_ap, r_key: apply_mod(l, 2, t_, b, r_ap, r_key))})
            for og in range(2):
                slot, rkey = acquire((p, l, ("mo", og, t)))
                sv = slot[:, 0:4096].rearrange("p (k w) -> p k w", w=512)
                for mm in range(4):
                    m = og * 4 + mm
                    o_ap, o_key = ps_r.next()
                    tr.mmg_fine([(o_ap, sv[:, k, mm * 128:(mm + 1) * 128], yn[:, k, :], k == 0, k == DC - 1)
                                 for k in range(DC)], [[("yn", k)] for k in range(DC)], [rkey], [o_key])
                    tr.op("dve", lambda e, o_ap=o_ap, m=m, xa=xs(m, t): e.scalar_tensor_tensor(
                        out=xa, in0=o_ap, scalar=Gsc[:, l, 1, m, b:b + 1], in1=xa,
                        op0=ALU.mult, op1=ALU.add), reads=[o_key, ("G",)] + xkeys([m], t), writes=xkeys([m], t))
                    lag.step(m, t)
            lag.flush()

        next_acc = {}

        def load_tile(p, t):
            buf = xt[t][p % 2]
            bid = t * 2 + p % 2
            tr.dma("sp", [lambda e, buf=buf, t=t, p=p: e.dma_start(out=buf[:, :, :], in_=x_d[p][:, :, t * TT:(t + 1) * TT])],
                   f"d_x{bid}", writes=[("x", bid, c) for c in range(DC)])

        def final_tile(p, t, r_ap, r_key):
            for c in range(DC):
                tr.op("dve", lambda e, c=c, xa=xs(c, t): e.scalar_tensor_tensor(
                    out=xa, in0=xa, scalar=pcol(P_FNG + c), in1=r_ap,
                    op0=ALU.mult, op1=ALU.mult), reads=xkeys([c], t) + [r_key, ("P",)], writes=xkeys([c], t))
            tr.dma("sp", [lambda e, src_all=xt[t][p % 2][:, :, :], t=t, p=p: e.dma_start(
                out=y_d[p][:, :, t * TT:(t + 1) * TT], in_=src_all)],
                f"d_out{t}", reads=xkeys(range(DC), t), writes=[("yout", t)])

        load_tile(0, 0)
        load_tile(0, 1)
        for p in range(npass):
            cur["p"] = p
            b = p // 2
            half = p % 2
            hooks0 = {}
            if p + 1 < npass:
                hooks0 = {3: (lambda p=p: load_tile(p + 1, 0)), 7: (lambda p=p: load_tile(p + 1, 1))}
            for t in range(NT):
                if t in next_acc:
                    assert next_acc.pop(t).n == DC
                else:
                    norm_mod(0, 0, t, b)
            for l in range(nl):
                ffn(p, l, 0, b, pre_normed=True, hooks=(hooks0 if l == 0 else None))
                tr.fence("u", ("a",))
                g0 = mixer_tile(p, l, 0, b, half)
                g1 = mixer_tile(p, l, 1, b, half)

                def run(g, until):
                    for tag in g:
                        if tag == until:
                            return
                run(g0, "uS")
                run(g1, "v1")
                run(g0, "normA")
                run(g1, "c_v2")
                run(g0, None)
                run(g1, None)
                ffn(p, l, 1, b, pre_normed=True)

        out_sems = [(s, v) for s, v in tr.dma_counts.items() if s.startswith("d_out") or s.startswith("d_wb")]

        for E in tr.engs.values():
            for waits, fn, inc in E["ops"]:
                for s, v in waits:
                    tr.sem(s)
                if inc is not None:
                    tr.sem(inc[0])

        _DBG['tr'] = tr
        block = es.enter_context(nc.Block())

        @block.tensor
        def _(e):
            tr.replay("pe", e)

        @block.scalar
        def _(e):
            tr.replay("act", e)

        @block.vector
        def _(e):
            tr.replay("dve", e)

        @block.gpsimd
        def _(e):
            tr.replay("pool", e)

        @block.sync
        def _(e):
            tr.replay("sp", e)
            for s, v in out_sems:
                e.wait_ge(tr.sem(s), v)
    return nc


def _pack_params(inp):
    par = np.zeros((128, NPARAM), np.float32)

    def fm(v):
        v = np.asarray(v, np.float32)
        n = v.shape[-1] // 128
        lead = v.shape[:-1]
        return np.moveaxis(v.reshape(lead + (n, 128)), -1, 0)

    ng = np.stack([inp["norm_ffn1_g"], inp["norm_mix_g"], inp["norm_ffn2_g"]], axis=1)
    par[:, P_NG:P_NG + 48] = fm(ng).reshape(128, 48)
    par[:, P_FNG:P_FNG + 8] = fm(inp["final_norm_g"]).reshape(128, 8)
    par[:, P_ONG:P_ONG + 16] = fm(inp["out_norm_g"]).reshape(128, 16)
    par[:, P_CW:P_CW + 24] = fm(inp["conv_w"]).reshape(128, 24)
    par[:, P_ADAB:P_ADAB + 144] = fm(inp["ada_b"]).reshape(128, 144)
    lng = np.asarray(inp["sgu_ln_g"], np.float32)
    lnb = np.asarray(inp["sgu_ln_b"], np.float32)
    par[:, P_LNGC:P_LNGC + 2] = np.tile(lng, (1, 2)).T
    par[:, P_LNBC:P_LNBC + 2] = np.tile(lnb, (1, 2)).T
    bsv = np.asarray(inp["sgu_b"], np.float32)
    bs = np.repeat(bsv.reshape(2, 4, 2, 1, 128), 64, axis=3).reshape(2, 4, 128, 128)
    par[:, P_BS:P_BS + 1024] = np.transpose(bs, (2, 0, 1, 3)).reshape(128, 1024)
    return par


def make_in_maps(inp, n_cores=N_CORES, npass=8):
    par = _pack_params(inp)
    wm = np.ascontiguousarray(np.transpose(np.asarray(inp["sgu_w_s"], np.float32), (3, 0, 1, 2)))
    shared = {
        "params": par, "wm": wm,
        "ada_w": np.ascontiguousarray(inp["ada_w"], dtype=np.float32),
        "ffn1_w_gu": np.ascontiguousarray(inp["ffn1_w_gu"], dtype=np.float32),
        "ffn2_w_gu": np.ascontiguousarray(inp["ffn2_w_gu"], dtype=np.float32),
        "ffn1_w_down": np.ascontiguousarray(inp["ffn1_w_down"], dtype=np.float32),
        "ffn2_w_down": np.ascontiguousarray(inp["ffn2_w_down"], dtype=np.float32),
        "mix_w_in": np.ascontiguousarray(inp["mix_w_in"], dtype=np.float32),
        "mix_w_out": np.ascontiguousarray(inp["mix_w_out"], dtype=np.float32),
    }
    x = np.asarray(inp["x"], np.float32)
    c = np.asarray(inp["c"], np.float32)
    maps = []
    for i in range(n_cores):
        xi = x[BPC * i:BPC * (i + 1)].reshape(BPC * SEQ, D)[:npass * PT]
        xi = np.ascontiguousarray(xi.reshape(npass, PT, DC, 128).transpose(0, 3, 2, 1))
        ci = c[BPC * i:BPC * (i + 1)]
        cT = np.ascontiguousarray(np.transpose(ci.T.reshape(DC, 128, BPC), (1, 0, 2)))
        m = dict(shared)
        m["x"] = xi
        m["cT"] = cT
        maps.append(m)
    return maps


def unpack_y(y, npass=8):
    return np.asarray(y, np.float32).reshape(npass, 128, DC, PT).transpose(0, 3, 2, 1).reshape(npass * PT, D)


_NC_CACHE = {}


def kernel(**inputs):
    if "full" not in _NC_CACHE:
        _NC_CACHE["full"] = build_program(8, 2)
    nc = _NC_CACHE["full"]
    in_maps = make_in_maps(inputs)
    res = run_bass_kernel_spmd(nc, in_maps, core_ids=list(range(N_CORES)))
    outs = [unpack_y(r["y"]).reshape(BPC, SEQ, D) for r in res.results]
    return np.concatenate(outs, axis=0)
```
